# Optimizing a Trainium2 kernel written in Bass

```python
import jax, jax.numpy as jnp
from jax import lax
import numpy as np

D_MODEL = 2048
BATCH = 4
SEQ = 2048
DEPTH = 1
DEC_BATCH = 128
DEC_SEQ = 8
PAST_LEN = 16384
PAGE_SIZE = 128

N_META = 16
D_HALF = D_MODEL // 2
DK_A = 128
DV_A = 128
H_A = D_HALF // DV_A
H_B = 4
DV_B = D_HALF // H_B
DK_B = DV_B // 2
ALPHA_RANK = 16
GATE_TEMP = 16.0
D_MIX = H_A * DV_A + H_B * DV_B
D_FF = 5632
CHUNK = 64
EPS = 1e-6
PROJ_WIDTHS = (H_A * DK_A, H_A * DK_A, H_A * DV_A, H_A * DV_A,
               H_B * DK_B, H_B * DK_B, H_B * DV_B, H_B * DV_B, ALPHA_RANK)
D_IN_PROJ = sum(PROJ_WIDTHS)
SPLIT_IDX = [int(v) for v in np.cumsum(PROJ_WIDTHS)[:-1]]

kernel_name = "hymba_hgrn2_gla_macaron_step"


def rmsnorm(x, w):
    xf = x.astype(jnp.float32)
    y = xf * lax.rsqrt(jnp.mean(xf * xf, axis=-1, keepdims=True) + EPS)
    return (y * w.astype(jnp.float32)).astype(x.dtype)


def head_rmsnorm(o, w):
    y = o * lax.rsqrt(jnp.mean(o * o, axis=-1, keepdims=True) + EPS)
    b, t = o.shape[:2]
    return y.reshape(b, t, -1) * w.astype(jnp.float32)


def swiglu(h, norm_w, w_in, w_out):
    u = rmsnorm(h, norm_w)
    g, up = jnp.split(u @ w_in, 2, axis=-1)
    return (jax.nn.silu(g) * up) @ w_out


def gated_linear_scan(q, k, v, logg, S0):
    b, t, h, dk = q.shape
    dv = v.shape[-1]
    c = min(CHUNK, t)
    n = -(-t // c)
    pad = n * c - t
    padf = lambda a: jnp.pad(a, ((0, 0), (0, pad), (0, 0), (0, 0)))
    to_chunks = lambda a: padf(a).reshape(b, n, c, h, a.shape[-1]).transpose(1, 0, 2, 3, 4)
    mask = jnp.tril(jnp.ones((c, c), dtype=bool))[None, :, :, None, None]

    def step(S, inp):
        qc, kc, vc, gc = inp
        Bc = jnp.cumsum(gc, axis=1)
        o_inter = jnp.einsum('bthk,bhkv->bthv', qc * jnp.exp(Bc), S)
        diff = Bc[:, :, None] - Bc[:, None, :]
        decay = jnp.exp(jnp.where(mask, diff, -jnp.inf))
        A = jnp.einsum('bthk,bshk,btshk->bhts', qc, kc, decay)
        o_intra = jnp.einsum('bhts,bshv->bthv', A, vc)
        Blast = Bc[:, -1]
        kdec = kc * jnp.exp(Blast[:, None] - Bc)
        S_new = jnp.exp(Blast)[..., None] * S + jnp.einsum('bshk,bshv->bhkv', kdec, vc)
        return S_new, o_inter + o_intra

    S, o = lax.scan(step, S0.astype(jnp.float32), (to_chunks(q), to_chunks(k), to_chunks(v), to_chunks(logg)))
    o = o.transpose(1, 0, 2, 3, 4).reshape(b, n * c, h, dv)[:, :t]
    return o, S


def segmented_scan(q, k, v, logg, S0, seg_lens):
    outs, S, start = [], S0, 0
    for L in seg_lens:
        sl = slice(start, start + L)
        o, S = gated_linear_scan(q[:, sl], k[:, sl], v[:, sl], logg[:, sl], S)
        outs.append(o)
        start += L
    return jnp.concatenate(outs, axis=1), S


def token_mix(u, S_a0, S_b0, seg_lens, lb, w_in, w_alpha_up, b_alpha, gn_a, gn_b, w_out):
    b, t, _ = u.shape
    f32 = jnp.float32
    q_a, f_a, i_a, g_a, q_b, k_b, v_b, r_b, a_low = jnp.split(u @ w_in, SPLIT_IDX, axis=-1)
    heads = lambda a, h: a.astype(f32).reshape(b, t, h, -1)
    lbh = lb.reshape(H_A, DK_A)
    fg = lbh + (1.0 - lbh) * jax.nn.sigmoid(heads(f_a, H_A))
    qa = heads(jax.nn.silu(q_a), H_A) * (DK_A ** -0.5)
    o_a, S_a = segmented_scan(qa, 1.0 - fg, heads(i_a, H_A), jnp.log(fg), S_a0, seg_lens)
    logg_b = heads(jax.nn.log_sigmoid((a_low @ w_alpha_up + b_alpha).astype(f32)) / GATE_TEMP, H_B)
    qb = heads(q_b, H_B) * (DK_B ** -0.5)
    o_b, S_b = segmented_scan(qb, heads(k_b, H_B), heads(v_b, H_B), logg_b, S_b0, seg_lens)
    o_a = head_rmsnorm(o_a, gn_a) * jax.nn.silu(g_a.astype(f32))
    o_b = head_rmsnorm(o_b, gn_b) * jax.nn.silu(r_b.astype(f32))
    o = jnp.concatenate([o_a, o_b], axis=-1).astype(u.dtype)
    return o @ w_out, S_a, S_b


def trunk(h, states_a, states_b, seg_lens, lb_logits,
          ffn1_norm, w_ffn1_in, w_ffn1_out, mix_norm, w_in, w_alpha_up, b_alpha, gnorm_a, gnorm_b, w_out,
          ffn2_norm, w_ffn2_in, w_ffn2_out, final_norm):
    lbs = jnp.cumsum(jax.nn.softmax(lb_logits.astype(jnp.float32), axis=0), axis=0)
    new_a, new_b = [], []
    for l in range(DEPTH):
        h = h + 0.5 * swiglu(h, ffn1_norm[l], w_ffn1_in[l], w_ffn1_out[l])
        m, Sa, Sb = token_mix(rmsnorm(h, mix_norm[l]), states_a[l], states_b[l], seg_lens, lbs[l],
                              w_in[l], w_alpha_up[l], b_alpha[l], gnorm_a[l], gnorm_b[l], w_out[l])
        h = h + m
        h = h + 0.5 * swiglu(h, ffn2_norm[l], w_ffn2_in[l], w_ffn2_out[l])
        new_a.append(Sa)
        new_b.append(Sb)
    return rmsnorm(h, final_norm), jnp.stack(new_a), jnp.stack(new_b)


def setup_inputs(seed: int = 0) -> dict:
    key = jax.random.key(seed)
    ks = jax.random.split(key, 24)
    nrm = lambda k, shape, s: jax.random.normal(k, shape, jnp.float32) * s
    gain = lambda k, shape: 1.0 + nrm(k, shape, 0.02)
    return {
        "x_prompt": nrm(ks[0], (BATCH, SEQ, D_MODEL), 1.0),
        "x_sample": nrm(ks[1], (DEC_BATCH, DEC_SEQ, D_MODEL), 1.0),
        "state_hgrn": nrm(ks[2], (DEPTH, DEC_BATCH, H_A, DK_A, DV_A), 0.5),
        "state_gla": nrm(ks[3], (DEPTH, DEC_BATCH, H_B, DK_B, DV_B), 0.5),
        "meta_tokens": nrm(ks[4], (N_META, D_MODEL), 1.0),
        "lb_logits": nrm(ks[5], (DEPTH + 1, H_A * DK_A), 0.5),
        "ffn1_norm": gain(ks[6], (DEPTH, D_MODEL)),
        "w_ffn1_in": nrm(ks[7], (DEPTH, D_MODEL, 2 * D_FF), D_MODEL ** -0.5),
        "w_ffn1_out": nrm(ks[8], (DEPTH, D_FF, D_MODEL), D_FF ** -0.5),
        "mix_norm": gain(ks[9], (DEPTH, D_MODEL)),
        "w_in": nrm(ks[10], (DEPTH, D_MODEL, D_IN_PROJ), D_MODEL ** -0.5),
        "w_alpha_up": nrm(ks[11], (DEPTH, ALPHA_RANK, H_B * DK_B), ALPHA_RANK ** -0.5),
        "b_alpha": nrm(ks[12], (DEPTH, H_B * DK_B), 0.1),
        "gnorm_a": gain(ks[13], (DEPTH, H_A * DV_A)),
        "gnorm_b": gain(ks[14], (DEPTH, H_B * DV_B)),
        "w_out": nrm(ks[15], (DEPTH, D_MIX, D_MODEL), D_MIX ** -0.5),
        "ffn2_norm": gain(ks[16], (DEPTH, D_MODEL)),
        "w_ffn2_in": nrm(ks[17], (DEPTH, D_MODEL, 2 * D_FF), D_MODEL ** -0.5),
        "w_ffn2_out": nrm(ks[18], (DEPTH, D_FF, D_MODEL), D_FF ** -0.5),
        "final_norm": gain(ks[19], (D_MODEL,)),
    }


def reference(x_prompt, x_sample, state_hgrn, state_gla, meta_tokens, lb_logits,
              ffn1_norm, w_ffn1_in, w_ffn1_out, mix_norm, w_in, w_alpha_up, b_alpha, gnorm_a, gnorm_b, w_out,
              ffn2_norm, w_ffn2_in, w_ffn2_out, final_norm):
    weights = (lb_logits, ffn1_norm, w_ffn1_in, w_ffn1_out, mix_norm, w_in, w_alpha_up, b_alpha,
               gnorm_a, gnorm_b, w_out, ffn2_norm, w_ffn2_in, w_ffn2_out, final_norm)
    b = x_prompt.shape[0]
    meta = jnp.broadcast_to(meta_tokens.astype(x_prompt.dtype)[None], (b, N_META, D_MODEL))
    h_p = jnp.concatenate([meta, x_prompt], axis=1)
    za = jnp.zeros((DEPTH, b, H_A, DK_A, DV_A), jnp.float32)
    zb = jnp.zeros((DEPTH, b, H_B, DK_B, DV_B), jnp.float32)
    out_p, new_a_p, new_b_p = trunk(h_p, za, zb, (N_META, x_prompt.shape[1]), *weights)
    y_prompt = out_p[:, N_META:]
    y_sample, new_a_s, new_b_s = trunk(x_sample, state_hgrn, state_gla, (x_sample.shape[1],), *weights)
    return (y_prompt, y_sample, new_a_p, new_b_p, new_a_s, new_b_s)
```

```python
import os
import numpy as np
from contextlib import ExitStack
import concourse.bass as bass
import concourse.mybir as mybir
from concourse.bass_utils import run_bass_kernel_spmd

F32 = mybir.dt.float32
BF16 = mybir.dt.bfloat16
AF = mybir.ActivationFunctionType
ALU = mybir.AluOpType

D = 2048
DC = 16
NT = 1160
NP = 1032
DFF = 5632
FC = 44
N_META = 16
EPS = 1e-6
TT = [(0, 512), (512, 512), (1024, 136)]
STILES = [(0, 128, "S"), (128, 16, "P")] + [(144 + 128 * i, 128, "P") for i in range(7)] + [(1040, 120, "P")]
WSLOTS = 4
SKIP = set(os.environ.get('KSKIP', '').split(','))
WBLK = 4096
NDMASEM = 16


class Op:
    __slots__ = ("idx", "eng", "fn", "deps", "dma", "need_inc", "ticket", "nobar", "prevsame", "special")

    def __init__(self, idx, eng, fn, dma, nobar):
        self.idx = idx
        self.eng = eng
        self.fn = fn
        self.dma = dma
        self.deps = set()
        self.need_inc = False
        self.ticket = None
        self.nobar = nobar
        self.prevsame = None
        self.special = False


class Sched:
    ENGS = ("pe", "act", "dve", "pool", "sp")

    def __init__(self):
        self.ops = []
        self.last_w = {}
        self.readers = {}
        self.last_on_eng = {e: None for e in self.ENGS}
        self.pending_bar = {e: set() for e in self.ENGS}
        self.dma_since_bar = []

    def op(self, eng, fn, reads=(), writes=(), dma=False, nobar=False, special=False):
        o = Op(len(self.ops), eng, fn, dma or special, nobar)
        o.special = special
        deps = o.deps
        for r in reads:
            w = self.last_w.get(r)
            if w is not None:
                deps.add(w)
        for r in writes:
            w = self.last_w.get(r)
            if w is not None:
                deps.add(w)
            rd = self.readers.get(r)
            if rd:
                for k, v in rd.items():
                    if k == "dma":
                        deps.update(v)
                    else:
                        deps.add(v)
        if self.pending_bar[eng] and not nobar:
            deps.update(self.pending_bar[eng])
            self.pending_bar[eng] = set()
        for r in reads:
            rd = self.readers.setdefault(r, {})
            if dma:
                rd.setdefault("dma", []).append(o.idx)
            else:
                rd[eng] = o.idx
        for r in writes:
            self.last_w[r] = o.idx
            self.readers[r] = {}
        deps.discard(o.idx)
        self.ops.append(o)
        if not dma:
            self.last_on_eng[eng] = o.idx
        elif not nobar:
            self.dma_since_bar.append(o.idx)
        return o

    def barrier(self):
        s = set(self.dma_since_bar)
        for e in self.ENGS:
            if self.last_on_eng[e] is not None:
                s.add(self.last_on_eng[e])
        self.dma_since_bar = []
        for e in self.ENGS:
            self.pending_bar[e] = set(s) | self.pending_bar[e]

    def emit(self, nc, block, es):
        ops = self.ops
        for o in ops:
            best = {}
            keep = set()
            for d in o.deps:
                p = ops[d]
                if p.dma:
                    keep.add(d)
                else:
                    if p.eng == o.eng and not o.dma and o.eng == "pe":
                        continue
                    if p.eng not in best or best[p.eng] < d:
                        best[p.eng] = d
            keep.update(best.values())
            o.deps = keep
            for d in keep:
                ops[d].need_inc = True
        lasts = {e: self.last_on_eng[e] for e in self.ENGS}
        for e in self.ENGS:
            if lasts[e] is not None:
                ops[lasts[e]].need_inc = True
        sems = {e: es.enter_context(nc.semaphore("s_" + e)) for e in self.ENGS}
        qbase = {}
        nsem = 0
        for e in self.ENGS:
            qbase[e] = nsem
            nsem += NDMASEM
        dsems = [es.enter_context(nc.semaphore("d%d" % i)) for i in range(nsem)]
        cnt = {e: 0 for e in self.ENGS}
        dcnt = [0] * nsem
        dlast = [None] * nsem
        nd = {e: 0 for e in self.ENGS}
        for o in ops:
            if o.special:
                sp_sem = es.enter_context(nc.semaphore("cc%d" % o.idx))
                dsems.append(sp_sem)
                o.ticket = (len(dsems) - 1, 1, None)
            elif o.dma:
                k = qbase[o.eng] + nd[o.eng] % NDMASEM
                nd[o.eng] += 1
                dcnt[k] += 16
                o.ticket = (k, dcnt[k], 16)
                o.prevsame = dlast[k]
                dlast[k] = o.idx
            elif o.need_inc:
                cnt[o.eng] += 1
                o.ticket = (o.eng, cnt[o.eng], 1)
        per_eng = {e: [o for o in ops if o.eng == e] for e in self.ENGS}
        all_dma = [o for o in ops if o.dma]
        self.stats = {e: len(per_eng[e]) for e in self.ENGS}

        def semof(key):
            return sems[key] if isinstance(key, str) else dsems[key]

        def run(ename, eng):
            known = {}

            def wait(t):
                key, val, _ = t
                if known.get(key, 0) >= val:
                    return
                known[key] = val
                eng.wait_ge(semof(key), val)

            for o in per_eng[ename]:
                for d in sorted(o.deps):
                    wait(ops[d].ticket)
                if o.dma and o.prevsame is not None:
                    wait(ops[o.prevsame].ticket)
                ins = o.fn(eng)
                if o.ticket is not None:
                    if o.ticket[2] is None:
                        ins.then_inc(semof(o.ticket[0]))
                    else:
                        ins.then_inc(semof(o.ticket[0]), o.ticket[2])
            if ename == "sp":
                for o in all_dma:
                    wait(o.ticket)
                for e in self.ENGS:
                    if lasts[e] is not None:
                        wait(ops[lasts[e]].ticket)

        block.tensor(lambda e: run("pe", e))
        block.scalar(lambda e: run("act", e))
        block.vector(lambda e: run("dve", e))
        block.gpsimd(lambda e: run("pool", e))
        block.sync(lambda e: run("sp", e))


class Arena:
    def __init__(self, t, nbytes, base4=0):
        self.t = t
        self.n4 = base4 + nbytes // 4
        self.off = base4

    def alloc(self, shape, dtype, at=None):
        free = int(np.prod(shape[1:]))
        esz = 4 if dtype == F32 else 2
        n4 = (free * esz + 3) // 4
        n4 = (n4 + 7) // 8 * 8
        if at is None:
            at = self.off
            self.off += n4
            assert self.off <= self.n4, ("arena overflow", self.off * 4)
        v = self.t[:, at:at + n4]
        if dtype != F32:
            v = v.bitcast(dtype)
        v = v[:, 0:free]
        if len(shape) == 3:
            v = v.rearrange("p (a b) -> p a b", a=shape[1])
        elif len(shape) == 4:
            v = v.rearrange("p (a b c) -> p a b c", a=shape[1], b=shape[2])
        if shape[0] < 128:
            v = v[0:shape[0]]
        return v

    def mark(self):
        return self.off

    def reset(self, m):
        self.off = m


def build_program(stage=99, use_cc=True):
    nc = bass.Bass("TRN2", target_bir_lowering=False)
    S = Sched()
    es = ExitStack()

    def din(name, shape, dt=F32):
        return nc.dram_tensor(name, list(shape), dt, kind="ExternalInput").ap()

    def dout(name, shape, dt=F32):
        return nc.dram_tensor(name, list(shape), dt, kind="ExternalOutput").ap()

    xin = din("xin", [NT, D])
    cst = din("cst", [128, 672])
    nrm = din("nrm", [4, D])
    w1i = din("w1i", [D, 2 * DFF])
    w1o = din("w1o", [DFF, D])
    w2i = din("w2i", [D, 2 * DFF])
    w2o = din("w2o", [DFF, D])
    y = dout("y", [NT, D])
    wmi = din("wmi", [D, 7184])
    wal_d = din("wal", [17, 512])
    lbl = din("lbl", [2, 1024])
    gn_d = din("gn", [2, 1024])
    wmo = din("wmo", [D, D])
    stA = din("stA", [16, 8, 128, 128])
    stB = din("stB", [16, 4, 128, 256])
    nsA = dout("nsA", [16, 8, 128, 128])
    nsB = dout("nsB", [16, 4, 128, 256])
    npA = dout("npA", [8, 128, 128])
    npB = dout("npB", [4, 128, 256])
    inb = nc.dram_tensor("inb", [128, 2048], F32)
    outb = nc.dram_tensor("outb", [256, 2048], F32)

    def scr(name, shape, dt):
        if stage < 99:
            return nc.dram_tensor(name, list(shape), dt, kind="ExternalOutput").ap()
        return nc.dram_tensor(name, list(shape), dt).ap()

    FM = scr("FM", [40, 128, NT], BF16)
    lgA = scr("lgA", [2, NT, 1024], BF16)
    kA = scr("kA", [NT, 1024], BF16)
    vA = scr("vA", [NT, 1024], BF16)
    lgB = scr("lgB", [2, NT, 512], BF16)
    kB = scr("kB", [NT, 512], BF16)
    vB = scr("vB", [NT, 1024], BF16)

    arena_t = es.enter_context(nc.sbuf_tensor("arena", [128, 53000], F32))
    A = Arena(arena_t, 212000)
    psum = es.enter_context(nc.psum_tensor("ps", [128, 8, 512], F32))

    h = A.alloc([128, DC, NT], F32)
    u = A.alloc([128, DC, NT], BF16)
    wring = [A.alloc([128, WBLK], BF16) for _ in range(WSLOTS)]
    cs = A.alloc([128, 672], F32)
    nw = A.alloc([128, 4, DC], F32)
    VOFF = A.mark()
    vecf = [A.alloc([128, NT], F32) for _ in range(2)]
    stg = [A.alloc([128, NT], BF16) for _ in range(3)]
    RX = A.mark()

    ident = cs[:, 0:128]
    ones_bf = A.alloc([128, 128], BF16)
    eps_t = A.alloc([128, 1], F32)
    RX = A.mark()

    pe_unit = [psum[:, 0:3, :], psum[:, 3:6, :]]

    def unit_flat(ui):
        return psum[:, 3 * ui:3 * ui + 3, :].rearrange("p a b -> p (a b)")[:, 0:NT]

    state = {"unit": 0, "stg": 0, "vec": 0}

    S.op("sp", lambda e: e.dma_start(out=cs, in_=cst), writes=["cs"], dma=True)
    with nc.allow_non_contiguous_dma(reason="small param load"):
        pass
    S.op("act", lambda e: e.dma_start(out=nw, in_=nrm.rearrange("n (c p) -> p n c", p=128),
                                      allow_slow_non_contiguous=True), writes=["nw"], dma=True)
    S.op("dve", lambda e: e.memset(ones_bf, 1.0), writes=["ones"])
    S.op("dve", lambda e: e.memset(eps_t, EPS), writes=["eps"])

    xst = [A.alloc([128, D], F32) for _ in range(2)]
    ttiles = [(i * 128, 128) for i in range(9)] + [(1152, 8)]
    for ti, (t0, n) in enumerate(ttiles):
        xb = xst[ti % 2]
        S.op("sp", (lambda e, xb=xb, t0=t0, n=n: e.dma_start(out=xb[0:n, :], in_=xin[t0:t0 + n, :])),
             writes=[("xst", ti % 2)], dma=True)
        for g in range(4):
            bank = 6 + (g % 2)

            def f_tr(e, xb=xb, n=n, g=g, bank=bank):
                ins = None
                for q in range(4):
                    c = 4 * g + q
                    ins = e.transpose(out=psum[:, bank, q * 128:q * 128 + n], in_=xb[0:n, c * 128:(c + 1) * 128],
                                      identity=ident[0:n, 0:n])
                return ins
            S.op("pe", f_tr, reads=[("xst", ti % 2), "cs"], writes=[("bank", bank)])

            def f_cp(e, g=g, bank=bank, t0=t0, n=n):
                src = psum[:, bank, :].rearrange("p (q t) -> p q t", q=4)[:, :, 0:n]
                return e.activation(out=h[:, 4 * g:4 * g + 4, t0:t0 + n], in_=src, func=AF.Copy)
            S.op("act", f_cp, writes=[("bank", bank)] + [("h", c) for c in range(4 * g, 4 * g + 4)])
    S.barrier()
    A.reset(RX)

    def rmsnorm(widx, dst, dst_res):
        ui = state["unit"]
        state["unit"] ^= 1
        ures = [("bank", 3 * ui + k) for k in range(3)]
        for c in range(DC):
            sb = stg[c % 3]
            if c % 2 == 0:
                S.op("act", lambda e, c=c, sb=sb: e.activation(out=sb, in_=h[:, c, :], func=AF.Square),
                     reads=[("h", c)], writes=[("stg", c % 3)])
            else:
                S.op("dve", lambda e, c=c, sb=sb: e.tensor_tensor(out=sb, in0=h[:, c, :], in1=h[:, c, :], op=ALU.mult),
                     reads=[("h", c)], writes=[("stg", c % 3)])

            def f_mm(e, c=c, sb=sb, ui=ui):
                ins = None
                for k, (t0, n) in enumerate(TT):
                    ins = e.matmul(psum[:, 3 * ui + k, 0:n], lhsT=ones_bf, rhs=sb[:, t0:t0 + n],
                                   start=(c == 0), stop=(c == DC - 1))
                return ins
            S.op("pe", f_mm, reads=[("stg", c % 3), "ones"], writes=ures)
        rt, rinv = vecf
        S.op("act", lambda e, ui=ui: e.activation(out=rt, in_=unit_flat(ui), func=AF.Sqrt, bias=eps_t[:, 0:1],
                                                  scale=1.0 / D),
             reads=["eps"], writes=ures + [("vec", 0)])
        S.op("dve", lambda e: e.reciprocal(out=rinv, in_=rt), reads=[("vec", 0)], writes=[("vec", 1)])
        for c in range(DC):
            S.op("dve", lambda e, c=c: e.scalar_tensor_tensor(out=dst[:, c, :], in0=h[:, c, :],
                                                               scalar=nw[:, widx, c:c + 1], in1=rinv,
                                                               op0=ALU.mult, op1=ALU.mult),
                 reads=[("h", c), ("vec", 1), "nw"], writes=[(dst_res, c)])

    wstate = {"n": 0}

    def wload(src, a, b):
        slot = wstate["n"] % WSLOTS
        wstate["n"] += 1
        dst = wring[slot][:, 0:a * b].rearrange("p (a b) -> p a b", a=a)
        S.op("pool", lambda e, dst=dst, src=src: e.dma_start(out=dst, in_=src), writes=[("w", slot)],
             dma=True, nobar=True)
        return slot, dst

    def next_unit():
        ui = state["unit"]
        state["unit"] ^= 1
        return ui, [("bank", 3 * ui + k) for k in range(3)]

    def lin_job(ui, ures, wtiles, acts, reads):
        nk = len(wtiles)

        def f(e):
            ins = None
            for k in range(nk):
                for b, (t0, n) in enumerate(TT):
                    ins = e.matmul(psum[0:wtiles[k].shape[1], 3 * ui + b, 0:n], lhsT=wtiles[k],
                                   rhs=acts[k][:, t0:t0 + n], start=(k == 0), stop=(k == nk - 1))
            return ins
        S.op("pe", f, reads=reads, writes=ures)

    U_ALL = [("u", c) for c in range(DC)]

    def ffn(widx, w_in, w_out, hid):
        rmsnorm(widx, u, "u")
        w_in_v = w_in.rearrange("(k p) n -> p k n", p=128)
        w_out_v = w_out.rearrange("(f p) n -> p f n", p=128)
        NG = FC // 4

        def up_group(g):
            hb = hid[g % 2]
            for pr in range(2):
                j0 = 4 * g + 2 * pr
                sl_g, wg = wload(w_in_v[:, :, j0 * 128:j0 * 128 + 256], 16, 256)
                sl_u, wu = wload(w_in_v[:, :, DFF + j0 * 128:DFF + j0 * 128 + 256], 16, 256)
                for q in range(2):
                    jj = 2 * pr + q
                    ui, ures = next_unit()
                    lin_job(ui, ures, [wg[:, k, q * 128:(q + 1) * 128] for k in range(DC)],
                            [u[:, k, :] for k in range(DC)], U_ALL + [("w", sl_g)])
                    vi = state["vec"]
                    state["vec"] ^= 1
                    sgb = vecf[vi]
                    S.op("act", lambda e, ui=ui, sgb=sgb: e.activation(out=sgb, in_=unit_flat(ui), func=AF.Silu),
                         writes=ures + [("vec", vi)])
                    ui2, ures2 = next_unit()
                    lin_job(ui2, ures2, [wu[:, k, q * 128:(q + 1) * 128] for k in range(DC)],
                            [u[:, k, :] for k in range(DC)], U_ALL + [("w", sl_u)])
                    S.op("dve", lambda e, ui2=ui2, sgb=sgb, hb=hb, jj=jj: e.tensor_tensor(
                        out=hb[:, jj, :], in0=unit_flat(ui2), in1=sgb, op=ALU.mult),
                        reads=[("vec", vi)], writes=ures2 + [("hid", g % 2, jj)])

        def down_group(g):
            hb = hid[g % 2]
            for half in range(2):
                sl, wd = wload(w_out_v[:, 4 * g:4 * g + 4, half * 1024:(half + 1) * 1024], 4, 1024)
                for dd in range(8):
                    d = half * 8 + dd
                    ui, ures = next_unit()
                    lin_job(ui, ures, [wd[:, f, dd * 128:(dd + 1) * 128] for f in range(4)],
                            [hb[:, f, :] for f in range(4)], [("hid", g % 2, f) for f in range(4)] + [("w", sl)])
                    S.op("dve", lambda e, ui=ui, d=d: e.scalar_tensor_tensor(
                        out=h[:, d, :], in0=unit_flat(ui), scalar=0.5, in1=h[:, d, :], op0=ALU.mult, op1=ALU.add),
                        writes=ures + [("h", d)])

        up_group(0)
        for g in range(NG):
            if g + 1 < NG:
                up_group(g + 1)
            down_group(g)

    hid = [A.alloc([128, 4, NT], BF16) for _ in range(2)]
    if stage >= 1:
        ffn(0, w1i, w1o, hid)
        S.barrier()
    A.reset(RX)

    def fm_idx(kind, i):
        if kind == "qa":
            return (i // 4) * 12 + i % 4
        if kind == "ka":
            return (i // 4) * 12 + 4 + i % 4
        if kind == "ga":
            return (i // 4) * 12 + 8 + i % 4
        if kind == "qb":
            return 24 + i
        if kind == "kb":
            return 28 + i
        return 32 + i

    def inproj():
        rmsnorm(1, u, "u")
        lb_bc = A.alloc([128, 1024], F32)
        oml_bc = A.alloc([128, 1024], F32)
        lbF = A.alloc([128, 2, 8], F32)
        lbf = A.alloc([128, 8], F32)
        omlf = A.alloc([128, 8], F32)
        tmf = [A.alloc([128, 256], F32) for _ in range(4)]
        tms = [A.alloc([128, 256], BF16) for _ in range(4)]
        aT = A.alloc([17, NT], F32)
        wal = A.alloc([17, 512], F32)
        lgs = [A.alloc([128, 512], F32) for _ in range(4)]
        thl = [A.alloc([128, 2, 256], BF16) for _ in range(4)]
        tmf2 = [A.alloc([128, 256], F32) for _ in range(4)]
        lgsb = [A.alloc([128, 2, 512], BF16) for _ in range(4)]
        S.op("sp", lambda e: e.dma_start(out=lb_bc, in_=lbl[0:1, :].partition_broadcast(128)),
             writes=["lb_bc"], dma=True)
        S.op("sp", lambda e: e.dma_start(out=oml_bc, in_=lbl[1:2, :].partition_broadcast(128)),
             writes=["oml_bc"], dma=True)
        S.op("sp", lambda e: e.dma_start(out=lbF, in_=lbl.rearrange("n (c p) -> p n c", p=128),
                                         allow_slow_non_contiguous=True), writes=["lbF"], dma=True)
        S.op("sp", lambda e: e.dma_start(out=wal, in_=wal_d), writes=["wal"], dma=True)
        S.op("dve", lambda e: e.tensor_tensor(out=lb_bc, in0=lb_bc, in1=oml_bc, op=ALU.subtract),
             reads=["oml_bc"], writes=["lb_bc"])
        S.op("act", lambda e: e.activation(out=oml_bc, in_=lb_bc, func=AF.Sigmoid, scale=-1.0),
             reads=["lb_bc"], writes=["oml_bc"])
        S.op("act", lambda e: e.activation(out=lb_bc, in_=lb_bc, func=AF.Sigmoid),
             reads=["oml_bc"], writes=["lb_bc"])
        S.op("dve", lambda e: e.tensor_tensor(out=lbf, in0=lbF[:, 0, :], in1=lbF[:, 1, :], op=ALU.subtract),
             reads=["lbF"], writes=["lbf"])
        S.op("act", lambda e: e.activation(out=omlf, in_=lbf, func=AF.Sigmoid, scale=-1.0),
             reads=["lbf"], writes=["omlf"])
        S.op("dve", lambda e: e.memset(aT, 1.0), writes=["aT"])

        w_v = wmi.rearrange("(k p) n -> p k n", p=128)
        st = {"stg": 0, "tm": 0, "bq": 0}
        TMBANKS = [6, 7, 2, 5]

        def fm_job(wb, q, kind, i, sl):
            ui, ures = next_unit()
            lin_job(ui, ures, [wb[:, k, q * 128:(q + 1) * 128] for k in range(DC)],
                    [u[:, k, :] for k in range(DC)], U_ALL + [("w", sl)])
            si = st["stg"] % 3
            st["stg"] += 1
            sb = stg[si]
            if kind == "fa":
                vi = state["vec"]
                state["vec"] ^= 1
                vb_ = vecf[vi]
                S.op("act", lambda e: e.activation(out=vb_, in_=unit_flat(ui), func=AF.Sigmoid, scale=-1.0),
                     writes=ures + [("vec", vi)])
                S.op("dve", lambda e: e.tensor_scalar(out=sb, in0=vb_, scalar1=omlf[:, i:i + 1], scalar2=None,
                                                      op0=ALU.mult),
                     reads=[("vec", vi), "omlf"], writes=[("stg", si)])
                idx = fm_idx("ka", i)
            else:
                func = AF.Silu if kind in ("qa", "ga", "rb") else AF.Copy
                S.op("act", lambda e: e.activation(out=sb, in_=unit_flat(ui), func=func),
                     writes=ures + [("stg", si)])
                idx = fm_idx(kind, i)
            S.op("sp", lambda e: e.dma_start(out=FM[idx], in_=sb), reads=[("stg", si)], writes=[("FM", idx)],
                 dma=True)

        def tm_jobs(wb, kind, c0, sl):
            if "tmfa" in SKIP and kind == "fa":
                return
            if "tmother" in SKIP and kind != "fa":
                return
            def pe_job(t0, n):
                bq = TMBANKS[st["bq"] % len(TMBANKS)]
                st["bq"] += 1
                pv = psum[0:n, bq, 0:256]

                def f(e):
                    ins = None
                    for k in range(DC):
                        ins = e.matmul(pv, lhsT=u[:, k, t0:t0 + n], rhs=wb[:, k, :], start=(k == 0),
                                       stop=(k == DC - 1))
                    return ins
                S.op("pe", f, reads=U_ALL + [("w", sl)], writes=[("bank", bq)])
                ti = st["tm"] % 4
                st["tm"] += 1
                return bq, pv, ti

            def fa_first(t0, n):
                bq, pv, ti = pe_job(t0, n)
                a_ = tmf[ti]
                ra = ("tmf", ti)
                S.op("act", lambda e: e.activation(out=a_[0:n], in_=pv, func=AF.Sigmoid), writes=[("bank", bq), ra])
                S.op("dve", lambda e: e.tensor_tensor(out=a_[0:n], in0=a_[0:n], in1=oml_bc[0:n, c0:c0 + 256],
                                                      op=ALU.mult), reads=[ra, "oml_bc"], writes=[ra])
                S.op("dve", lambda e: e.tensor_tensor(out=a_[0:n], in0=a_[0:n], in1=lb_bc[0:n, c0:c0 + 256],
                                                      op=ALU.add), reads=[ra, "lb_bc"], writes=[ra])
                return ti

            def fa_second(t0, n, ti):
                a_, b_, kb_, hl = tmf[ti], tmf2[ti], tms[ti], thl[ti]
                ra, rb_ = ("tmf", ti), ("tmf2", ti)
                S.op("act", lambda e: e.activation(out=b_[0:n], in_=a_[0:n], func=AF.Ln), reads=[ra], writes=[rb_])
                S.op("dve", lambda e: e.tensor_copy(out=hl[0:n, 0, :], in_=b_[0:n]), reads=[rb_], writes=[("thl", ti)])
                S.op("dve", lambda e: e.tensor_tensor(out=hl[0:n, 1, :], in0=b_[0:n], in1=hl[0:n, 0, :],
                                                      op=ALU.subtract), reads=[rb_], writes=[("thl", ti)])
                S.op("sp", lambda e: e.dma_start(
                    out=lgA[:, t0:t0 + n, c0:c0 + 256].rearrange("a t c -> t a c"), in_=hl[0:n]),
                    reads=[("thl", ti)], dma=True)
                S.op("dve", lambda e: e.tensor_scalar(out=kb_[0:n], in0=a_[0:n], scalar1=-1.0, scalar2=1.0,
                                                      op0=ALU.mult, op1=ALU.add), reads=[ra], writes=[("tms", ti)])
                S.op("sp", lambda e: e.dma_start(out=kA[t0:t0 + n, c0:c0 + 256], in_=kb_[0:n]),
                     reads=[("tms", ti)], dma=True)

            def other(t0, n):
                bq, pv, ti = pe_job(t0, n)
                dst = {"ia": vA, "kb": kB, "vb": vB}[kind]
                sb = tms[ti]
                S.op("act", lambda e: e.activation(out=sb[0:n], in_=pv, func=AF.Copy),
                     writes=[("bank", bq), ("tms", ti)])
                S.op("sp", lambda e: e.dma_start(out=dst[t0:t0 + n, c0:c0 + 256], in_=sb[0:n]),
                     reads=[("tms", ti)], dma=True)

            tl = [(t0, n) for (t0, n, _) in STILES]
            if kind == "fa":
                for c_ in range(0, len(tl), 4):
                    chunk = tl[c_:c_ + 4]
                    tis = [fa_first(t0, n) for (t0, n) in chunk]
                    for (t0, n), ti in zip(chunk, tis):
                        fa_second(t0, n, ti)
            else:
                for (t0, n) in tl:
                    other(t0, n)

        plan = [("qa", 0, 4, True, False), ("fa", 1024, 4, True, True), ("ia", 2048, 4, False, True),
                ("ga", 3072, 4, True, False), ("qb", 4096, 2, True, False), ("kb", 4608, 2, True, True),
                ("vb", 5120, 4, False, True), ("rb", 6144, 4, True, False)]
        for kind, col0, nb, do_fm, do_tm in plan:
            for b in range(nb):
                sl, wb = wload(w_v[:, :, col0 + b * 256:col0 + (b + 1) * 256], 16, 256)
                if do_fm:
                    for q in range(2):
                        fm_job(wb, q, kind, 2 * b + q, sl)
                if do_tm and "tm" not in SKIP:
                    tm_jobs(wb, kind, b * 256, sl)
        if "alow" in SKIP:
            return
        sl, wb = wload(w_v[:, :, 7168:7184], 16, 16)
        ui, ures = next_unit()
        lin_job(ui, ures, [wb[:, k, :] for k in range(DC)], [u[:, k, :] for k in range(DC)], U_ALL + [("w", sl)])
        S.op("act", lambda e: e.activation(out=aT[0:16, :], in_=unit_flat(ui)[0:16, :], func=AF.Copy),
             writes=ures + ["aT"])
        def lgb_first(j, t0, n):
            bank = TMBANKS[j]
            S.op("pe", lambda e: e.matmul(psum[0:n, bank, :], lhsT=aT[:, t0:t0 + n], rhs=wal, start=True, stop=True),
                 reads=["aT", "wal"], writes=[("bank", bank)])
            lb_ = lgs[j]
            S.op("act", lambda e: e.activation(out=lb_[0:n], in_=psum[0:n, bank, :], func=AF.Sigmoid),
                 writes=[("bank", bank), ("lgs", j)])

        def lgb_second(j, t0, n):
            lb_, hb_ = lgs[j], lgsb[j]
            S.op("act", lambda e: e.activation(out=lb_[0:n], in_=lb_[0:n], func=AF.Ln),
                 reads=[("lgs", j)], writes=[("lgs", j)])
            S.op("dve", lambda e: e.tensor_scalar(out=lb_[0:n], in0=lb_[0:n], scalar1=1.0 / 16.0, scalar2=None,
                                                  op0=ALU.mult), reads=[("lgs", j)], writes=[("lgs", j)])
            S.op("dve", lambda e: e.tensor_copy(out=hb_[0:n, 0, :], in_=lb_[0:n]),
                 reads=[("lgs", j)], writes=[("lgsb", j)])
            S.op("dve", lambda e: e.tensor_tensor(out=hb_[0:n, 1, :], in0=lb_[0:n], in1=hb_[0:n, 0, :],
                                                  op=ALU.subtract), reads=[("lgs", j)], writes=[("lgsb", j)])
            S.op("sp", lambda e: e.dma_start(out=lgB[:, t0:t0 + n, :].rearrange("a t c -> t a c"), in_=hb_[0:n]),
                 reads=[("lgsb", j)], dma=True)

        if "lgb" not in SKIP:
            tl = [(t0, n) for (t0, n, _) in STILES]
            for c_ in range(0, len(tl), 4):
                chunk = tl[c_:c_ + 4]
                for j, (t0, n) in enumerate(chunk):
                    lgb_first(j, t0, n)
                for j, (t0, n) in enumerate(chunk):
                    lgb_second(j, t0, n)

    if stage >= 2:
        inproj()
        S.barrier()
    A.reset(RX)

    ofin = u
    LP, UP, LSm, USm = "LP", "UP", "LS", "US"
    mkf = {"LP": cs[:, 128:256], "UP": cs[:, 256:384], "LS": cs[:, 384:512], "US": cs[:, 512:640]}
    mkb = {}
    BM = cs[:, 640:656]
    flag_t = cs[:, 656:657]
    GROUPS = [dict(g=0, V=128, lg=lgA, lc=0, k=kA, v=vA, vc=0, fmb=0, nfm=12),
              dict(g=1, V=128, lg=lgA, lc=512, k=kA, v=vA, vc=512, fmb=12, nfm=12),
              dict(g=2, V=256, lg=lgB, lc=0, k=kB, v=vB, vc=0, fmb=24, nfm=16)]

    def scan_phase():
        A2 = Arena(arena_t, 16240, VOFF)
        gnF = A.alloc([128, 2, 8], F32)
        S.op("sp", lambda e: e.dma_start(out=gnF, in_=gn_d.rearrange("n (c p) -> p n c", p=128),
                                         allow_slow_non_contiguous=True), writes=["gnF"], dma=True)
        Sst = A.alloc([128, 4, 256], F32)
        Sbf = A.alloc([128, 4, 256], BF16)
        NSLOT = 4
        work = []
        for sl in range(NSLOT):
            base = dict(eNB=A.alloc([128, 128], F32), eD=A.alloc([128, 128], F32), kt=A.alloc([128, 128], BF16),
                        sq=A.alloc([128, 2, 128], BF16), rt=A.alloc([128, 128], F32), tmp=A.alloc([128, 2, 128], F32))
            pars = []
            for par in range(2):
                d_ = dict(base)
                d_.update(eB=A.alloc([128, 128], F32), qt=A.alloc([128, 128], BF16), kdec=A.alloc([128, 128], BF16),
                          ATm=A.alloc([128, 128], BF16))
                pars.append(d_)
            work.append(pars)
        M0 = A.mark()
        cnt = {"ld": 0}

        def load_tile(G, t0, n, need_fm):
            b = cnt["ld"] % 2
            cnt["ld"] += 1
            V = G["V"]
            S.op("sp", lambda e: e.dma_start(
                out=lgt[b][0:n], in_=G["lg"][:, t0:t0 + n, G["lc"]:G["lc"] + 512].rearrange("a t c -> t a c")),
                writes=[("lgt", b)], dma=True)
            S.op("sp", lambda e: e.dma_start(out=ktm[b][0:n, :], in_=G["k"][t0:t0 + n, G["lc"]:G["lc"] + 512]),
                 writes=[("ktm", b)], dma=True)
            S.op("sp", lambda e: e.dma_start(out=vtm[b][0:n, 0:4 * V], in_=G["v"][t0:t0 + n, G["vc"]:G["vc"] + 4 * V]),
                 writes=[("vtm", b)], dma=True)
            if need_fm:
                fb = G["fmb"]
                for (c0, c1) in [(0, 4), (4, 8), (8, G["nfm"])]:
                    S.op("sp", lambda e, c0=c0, c1=c1: e.dma_start(
                        out=FMt[b][:, c0:c1, 0:n], in_=FM[fb + c0:fb + c1, :, t0:t0 + n].rearrange("c p t -> p c t")),
                        writes=[("FMt", b, c0)], dma=True)
            return b

        def head_core(G, i, b, n, sl, Lm, Um, main, par=0):
            W = work[sl][par]
            bA = ("bank", 2 * sl)
            lgh = lgt[b][0:n, 0, i * 128:(i + 1) * 128]
            lgl = lgt[b][0:n, 1, i * 128:(i + 1) * 128]
            ktok = ktm[b][0:n, i * 128:(i + 1) * 128]
            pA = 2 * sl
            Lmb, Umb = mkb[Lm], mkb[Um]

            def f1(e):
                e.matmul(psum[0:n, pA, 128:256], lhsT=Umb[0:n, 0:n], rhs=lgh, start=True, stop=False)
                e.matmul(psum[0:n, pA, 128:256], lhsT=Umb[0:n, 0:n], rhs=lgl, start=False, stop=True)
                e.matmul(psum[:, pA, 0:n], lhsT=lgh, rhs=Lmb[0:n, 0:n], start=True, stop=False)
                return e.matmul(psum[:, pA, 0:n], lhsT=lgl, rhs=Lmb[0:n, 0:n], start=False, stop=True)
            yield S.op("pe", f1, reads=[("lgt", b), "mkb"], writes=[bA])
            yield S.op("act", lambda e: e.activation(out=W["eD"][0:n, :], in_=psum[0:n, pA, 128:256], func=AF.Exp),
                 writes=[bA, ("eD", sl)])
            yield S.op("act", lambda e: e.activation(out=W["eB"][:, 0:n], in_=psum[:, pA, 0:n], func=AF.Exp),
                 writes=[bA, ("eB", sl, par)])
            yield S.op("pool" if main else "dve",
                       lambda e: e.tensor_tensor(out=W["kdec"][0:n, :], in0=ktok, in1=W["eD"][0:n, :], op=ALU.mult),
                 reads=[("ktm", b), ("eD", sl)], writes=[("kdec", sl, par)])
            if not main:
                return
            qc = i
            kc = 4 + i
            yield S.op("act", lambda e: e.activation(out=W["eNB"][:, 0:n], in_=psum[:, pA, 0:n], func=AF.Exp, scale=-1.0),
                 writes=[bA, ("eNB", sl)])
            yield S.op("dve", lambda e: e.scalar_tensor_tensor(out=W["qt"][:, 0:n], in0=FMt[b][:, qc, 0:n],
                                                          scalar=128.0 ** -0.5, in1=W["eB"][:, 0:n],
                                                          op0=ALU.mult, op1=ALU.mult),
                 reads=[("FMt", b, 0), ("eB", sl, par)], writes=[("qt", sl, par)])
            yield S.op("pool", lambda e: e.tensor_tensor(out=W["kt"][:, 0:n], in0=FMt[b][:, kc, 0:n], in1=W["eNB"][:, 0:n],
                                                   op=ALU.mult),
                 reads=[("FMt", b, 4), ("eNB", sl)], writes=[("kt", sl)])
            yield S.op("pe", lambda e: e.matmul(psum[0:n, pA, 256:256 + n], lhsT=W["kt"][:, 0:n], rhs=W["qt"][:, 0:n],
                                          start=True, stop=True),
                 reads=[("kt", sl), ("qt", sl, par)], writes=[bA])
            yield S.op("dve", lambda e: e.tensor_tensor(out=W["ATm"][0:n, 0:n], in0=psum[0:n, pA, 256:256 + n],
                                                   in1=mkf[Lm][0:n, 0:n], op=ALU.mult),
                 reads=["cs"], writes=[bA, ("ATm", sl, par)])

        def head_norm(G, i, b, t0, n, sl):
            W = work[sl][0]
            V = G["V"]
            nvh = V // 128
            pA, pB = 2 * sl, 2 * sl + 1
            bA, bB = ("bank", pA), ("bank", pB)
            ov = psum[:, pB, 0:256].rearrange("p (a t) -> p a t", a=2)
            yield S.op("act", lambda e: e.activation(out=W["sq"][:, 0:nvh, 0:n], in_=ov[:, 0:nvh, 0:n], func=AF.Square),
                 writes=[bB, ("sq", sl)])
            yield S.op("act", lambda e: e.activation(out=W["tmp"][:, 0:nvh, 0:n], in_=ov[:, 0:nvh, 0:n], func=AF.Copy),
                 writes=[bB] + [("tmp", sl, vh) for vh in range(nvh)])

            def fss(e):
                ins = None
                for vh in range(nvh):
                    ins = e.matmul(psum[:, pA, 384:384 + n], lhsT=ones_bf, rhs=W["sq"][:, vh, 0:n],
                                   start=(vh == 0), stop=(vh == nvh - 1))
                return ins
            yield S.op("pe", fss, reads=[("sq", sl), "ones"], writes=[bA])
            yield S.op("act", lambda e: e.activation(out=W["rt"][:, 0:n], in_=psum[:, pA, 384:384 + n], func=AF.Sqrt,
                                               bias=eps_t[:, 0:1], scale=1.0 / V),
                 reads=["eps"], writes=[bA, ("rt", sl)])
            yield S.op("dve", lambda e: e.reciprocal(out=W["rt"][:, 0:n], in_=W["rt"][:, 0:n]), writes=[("rt", sl)])
            for vh in range(nvh):
                if G["g"] < 2:
                    gcol = gnF[:, 0, 4 * G["g"] + i:4 * G["g"] + i + 1]
                    mixc = 4 * G["g"] + i
                    gatec = 8 + i
                else:
                    gcol = gnF[:, 1, 2 * i + vh:2 * i + vh + 1]
                    mixc = 8 + 2 * i + vh
                    gatec = 8 + 2 * i + vh
                yield S.op("dve", lambda e, vh=vh, gcol=gcol: e.scalar_tensor_tensor(
                    out=W["tmp"][:, vh, 0:n], in0=W["tmp"][:, vh, 0:n], scalar=gcol, in1=W["rt"][:, 0:n],
                    op0=ALU.mult, op1=ALU.mult),
                    reads=[("rt", sl), "gnF"], writes=[("tmp", sl, vh)])
                yield S.op("pool", lambda e, vh=vh, mixc=mixc, gatec=gatec: e.tensor_tensor(
                    out=ofin[:, mixc, t0:t0 + n], in0=W["tmp"][:, vh, 0:n], in1=FMt[b][:, gatec, 0:n], op=ALU.mult),
                    reads=[("tmp", sl, vh), ("FMt", b, 8)], writes=[("ofin", mixc)])

        def prompt_block(G, i, b, sl, s0, nb, main, par=0):
            V = G["V"]
            nvh = V // 128
            W = work[sl][par]
            pB = 2 * sl + 1
            bB = ("bank", pB)
            vtok = vtm[b][:, i * V:(i + 1) * V]
            if main:
                def fo(e):
                    ins = None
                    for vh in range(nvh):
                        o_ = psum[:, pB, vh * 128 + s0:vh * 128 + s0 + nb]
                        e.matmul(o_, lhsT=Sbf[:, i, vh * 128:(vh + 1) * 128], rhs=W["qt"][:, s0:s0 + nb],
                                 start=True, stop=False)
                        ins = e.matmul(o_, lhsT=vtok[s0:s0 + nb, vh * 128:(vh + 1) * 128],
                                       rhs=W["ATm"][s0:s0 + nb, s0:s0 + nb], start=False, stop=True)
                    return ins
                yield S.op("pe", fo, reads=[("Sbf", i), ("qt", sl, par), ("vtm", b), ("ATm", sl, par)], writes=[bB])
            yield S.op("pe", lambda e: e.matmul(psum[:, pB, 256:256 + V], lhsT=W["kdec"][s0:s0 + nb, :],
                                          rhs=vtok[s0:s0 + nb, :], start=True, stop=True),
                 reads=[("kdec", sl, par), ("vtm", b)], writes=[bB])
            yield S.op("dve", lambda e: e.scalar_tensor_tensor(
                out=Sst[:, i, 0:V], in0=Sst[:, i, 0:V], scalar=W["eB"][:, s0 + nb - 1:s0 + nb],
                in1=psum[:, pB, 256:256 + V], op0=ALU.mult, op1=ALU.add),
                reads=[("eB", sl, par)], writes=[bB, ("Sst", i)])
            if main:
                yield S.op("dve", lambda e: e.tensor_copy(out=Sbf[:, i, 0:V], in_=Sst[:, i, 0:V]),
                     reads=[("Sst", i)], writes=[("Sbf", i)])

        def prepass_tile(G, i, b, sl, blocks, par):
            V = G["V"]
            W = work[sl][par]
            pB = 2 * sl + 1
            bB = ("bank", pB)
            vtok = vtm[b][:, i * V:(i + 1) * V]
            pA = 2 * sl
            bA = ("bank", pA)
            regs = [psum[:, pB, 256:256 + V], psum[:, pA, 256:256 + V]]
            rres = [[bB], [bA]]

            def fp(e):
                ins = None
                for bi, (s0, nb) in enumerate(blocks):
                    ins = e.matmul(regs[bi], lhsT=W["kdec"][s0:s0 + nb, :], rhs=vtok[s0:s0 + nb, :],
                                   start=True, stop=True)
                return ins
            yield S.op("pe", fp, reads=[("kdec", sl, par), ("vtm", b)], writes=[bB] + ([bA] if len(blocks) > 1 else []))
            for bi, (s0, nb) in enumerate(blocks):
                yield S.op("dve", lambda e, bi=bi, s0=s0, nb=nb: e.scalar_tensor_tensor(
                    out=Sst[:, i, 0:V], in0=Sst[:, i, 0:V], scalar=W["eB"][:, s0 + nb - 1:s0 + nb],
                    in1=regs[bi], op0=ALU.mult, op1=ALU.add),
                    reads=[("eB", sl, par)], writes=rres[bi] + [("Sst", i)])

        def run_rr(gens):
            gens = list(gens)
            while gens:
                for g_ in list(gens):
                    try:
                        next(g_)
                    except StopIteration:
                        gens.remove(g_)

        def interleave(gens):
            gens = list(gens)
            while gens:
                for g_ in list(gens):
                    try:
                        yield next(g_)
                    except StopIteration:
                        gens.remove(g_)

        def prompt_pass(main):
            ptiles = [(t0, n) for (t0, n, kind) in STILES if kind == "P"]
            T = len(ptiles)
            seq = [(gi, ti) for gi in range(len(GROUPS)) for ti in range(T)]
            sres = [("Sst", i) for i in range(4)]
            bufs = {}
            entered = {}
            began = {}
            finished = {}
            C0 = {0: 0, 1: 512, 2: 1024}

            def enter(k):
                entered[k] = entered.get(k, 0) + 1
                if entered[k] == 1:
                    gi, ti = seq[k]
                    bufs[k] = load_tile(GROUPS[gi], ptiles[ti][0], ptiles[ti][1], main)

            def begin_group(gi):
                began[gi] = began.get(gi, 0) + 1
                if began[gi] != 1:
                    return
                G = GROUPS[gi]
                V = G["V"]
                c0 = C0[gi]
                if main:
                    S.op("sp", lambda e: e.dma_start(
                        out=Sst[:, :, 0:V], in_=outb.ap()[0:128, c0:c0 + 4 * V].rearrange("p (i v) -> p i v", i=4)),
                        reads=["outb"], writes=sres, dma=True)
                    S.op("dve", lambda e: e.tensor_scalar(out=Sst[:, :, 0:V], in0=Sst[:, :, 0:V], scalar1=flag_t,
                                                          scalar2=None, op0=ALU.mult), reads=["cs"], writes=sres)
                    S.op("act", lambda e: e.activation(out=Sbf[:, :, 0:V], in_=Sst[:, :, 0:V], func=AF.Copy),
                         reads=sres, writes=[("Sbf", i) for i in range(4)])
                else:
                    S.op("dve", lambda e: e.memset(Sst, 0.0), writes=sres)

            def finish_group(gi):
                finished[gi] = finished.get(gi, 0) + 1
                if finished[gi] != 4:
                    return
                G = GROUPS[gi]
                V = G["V"]
                c0 = C0[gi]
                if main:
                    dst = (npA[4 * gi:4 * gi + 4] if gi < 2 else npB).rearrange("i k v -> k i v")
                    S.op("sp", lambda e: e.dma_start(out=dst, in_=Sst[:, :, 0:V]), reads=sres, dma=True)
                else:
                    S.op("sp", lambda e: e.dma_start(
                        out=inb.ap()[:, c0:c0 + 4 * V].rearrange("p (i v) -> p i v", i=4), in_=Sst[:, :, 0:V]),
                        reads=sres, writes=["inb"], dma=True)

            def stream(i):
                def pre(k):
                    gi, ti = seq[k]
                    t0, n = ptiles[ti]
                    yield from head_core(GROUPS[gi], i, bufs[k], n, i, LP, UP, main, k % 2)

                def chain(k):
                    gi, ti = seq[k]
                    G = GROUPS[gi]
                    t0, n = ptiles[ti]
                    if ti == 0:
                        begin_group(gi)
                    blocks = [(0, min(64, n))] + ([(64, n - 64)] if n > 64 else [])
                    if main:
                        for (s0, nb) in blocks:
                            yield from prompt_block(G, i, bufs[k], i, s0, nb, main, k % 2)
                        yield from head_norm(G, i, bufs[k], t0, n, i)
                    else:
                        yield from prepass_tile(G, i, bufs[k], i, blocks, k % 2)
                    if ti == T - 1:
                        finish_group(gi)
                        yield None
                enter(0)
                yield from pre(0)
                for k in range(len(seq)):
                    gens = [chain(k)]
                    if k + 1 < len(seq):
                        enter(k + 1)
                        gens.append(pre(k + 1))
                    yield from interleave(gens)
            run_rr([stream(i) for i in range(4)])

        sbufs = {}

        def sample_stream(G, b, sl):
            V = G["V"]
            nvh = V // 128
            ng = 512 // V
            nq = 16 // ng
            gi = G["g"]
            W = work[sl][0]
            pB = 2 * sl + 1
            bB = ("bank", pB)
            src_t = stA if gi < 2 else stB
            dst_t = nsA if gi < 2 else nsB
            eBl = W["eB"].rearrange("p (j t) -> p j t", t=8)[:, :, 7]
            S0b = sbufs["S0b"][sl]
            Vb = sbufs["Vbk"][sl]
            units = [(i, q) for i in (sl, sl + 2) for q in range(nq)]

            def hd_of(i):
                return (4 * gi + i) if gi < 2 else i

            def load(idx):
                i, q = units[idx]
                r = idx % 2
                sv = sbufs["S0g"][sl][r].rearrange("p (j v) -> p j v", j=ng)
                return S.op("sp", lambda e: e.dma_start(
                    out=sv, in_=src_t[q * ng:(q + 1) * ng, hd_of(i)].rearrange("j k v -> k j v")),
                    writes=[("S0g", sl, r)], dma=True)

            def compute(idx):
                i, q = units[idx]
                r = idx % 2
                j0 = q * ng
                pk = 4 + 2 * sl + r
                S0 = sbufs["S0g"][sl][r]
                sv = S0.rearrange("p (j v) -> p j v", j=ng)
                vtok = vtm[b][:, i * V:(i + 1) * V]
                rS = ("S0g", sl, r)
                yield S.op("act", lambda e: e.activation(out=S0b, in_=S0, func=AF.Copy),
                           reads=[rS], writes=[("S0b", sl)])

                def fj(e):
                    ins = None
                    for jj in range(ng):
                        j = j0 + jj
                        for vh in range(nvh):
                            ins = e.matmul(psum[:, pB, vh * 128 + 8 * j:vh * 128 + 8 * j + 8],
                                           lhsT=S0b[:, jj * V + vh * 128:jj * V + (vh + 1) * 128],
                                           rhs=W["qt"][:, 8 * j:8 * j + 8], start=False,
                                           stop=(q == nq - 1 and jj == ng - 1 and vh == nvh - 1),
                                           skip_group_check=True)
                    return ins
                yield S.op("pe", fj, reads=[("S0b", sl), ("qt", sl, 0)], writes=[bB])
                vb3 = Vb.rearrange("p (j v) -> p j v", j=ng)
                yield S.op("pool", lambda e: e.tensor_tensor(
                    out=vb3, in0=vtok.unsqueeze(1).broadcast_to([128, ng, V]),
                    in1=BM[:, j0:j0 + ng].unsqueeze(2).broadcast_to([128, ng, V]), op=ALU.mult),
                    reads=[("vtm", b), "cs"], writes=[("Vbk", sl)])
                yield S.op("pe", lambda e: e.matmul(psum[:, pk, :], lhsT=W["kdec"], rhs=Vb, start=True, stop=True),
                           reads=[("kdec", sl, 0), ("Vbk", sl)], writes=[("bank", pk)])
                yield S.op("dve", lambda e: e.tensor_tensor(
                    out=sv, in0=sv, in1=eBl[:, j0:j0 + ng].unsqueeze(2).broadcast_to([128, ng, V]), op=ALU.mult),
                    reads=[("eB", sl, 0)], writes=[rS])
                yield S.op("dve", lambda e: e.tensor_tensor(out=S0, in0=S0, in1=psum[:, pk, :], op=ALU.add),
                           writes=[("bank", pk), rS])
                yield S.op("sp", lambda e: e.dma_start(
                    out=dst_t[j0:j0 + ng, hd_of(i)].rearrange("j k v -> k j v"), in_=sv), reads=[rS], dma=True)

            def intra(i):
                vtok = vtm[b][:, i * V:(i + 1) * V]

                def fi(e):
                    ins = None
                    for vh in range(nvh):
                        ins = e.matmul(psum[:, pB, vh * 128:vh * 128 + 128], lhsT=vtok[:, vh * 128:(vh + 1) * 128],
                                       rhs=W["ATm"], start=(vh == 0), stop=False, skip_group_check=True)
                    return ins
                return S.op("pe", fi, reads=[("vtm", b), ("ATm", sl, 0)], writes=[bB])

            yield load(0)
            for idx, (i, q) in enumerate(units):
                if q == 0:
                    yield from head_core(G, i, b, 128, sl, LSm, USm, True)
                    yield intra(i)
                if idx + 1 < len(units):
                    yield load(idx + 1)
                yield from compute(idx)
                if q == nq - 1:
                    yield from head_norm(G, i, b, 0, 128, sl)

        def sample_pass():
            sbufs["S0g"] = [[A2.alloc([128, 512], F32) for _ in range(2)] for _ in range(2)]
            sbufs["S0b"] = [A2.alloc([128, 512], BF16) for _ in range(2)]
            sbufs["Vbk"] = [A2.alloc([128, 512], BF16) for _ in range(2)]
            for G in GROUPS:
                b = load_tile(G, 0, 128, True)
                run_rr([sample_stream(G, b, 0), sample_stream(G, b, 1)])

        lgt = [A.alloc([128, 2, 512], BF16) for _ in range(2)]
        mkb_t = A2.alloc([128, 512], BF16)
        S.op("dve", lambda e: e.tensor_copy(out=mkb_t, in_=cs[:, 128:640]), reads=["cs"], writes=["mkb"])
        for q_, nm_ in enumerate(("LP", "UP", "LS", "US")):
            mkb[nm_] = mkb_t[:, q_ * 128:(q_ + 1) * 128]
        ktm = [A.alloc([128, 512], BF16) for _ in range(2)]
        vtm = [A.alloc([128, 1024], BF16) for _ in range(2)]
        FMt = [A.alloc([128, 16, 128], BF16) for _ in range(2)]
        M1 = A.mark()
        prompt_pass(False)
        if use_cc:
            S.op("pool", lambda e: e.collective_compute(
                "AllGather", ALU.bypass, replica_groups=[[0, 1], [2, 3], [4, 5], [6, 7]],
                ins=[inb.ap().opt()], outs=[outb.ap().opt()]), reads=["inb"], writes=["outb"], special=True)
        else:
            S.op("sp", lambda e: e.dma_start(out=outb.ap()[0:128, :], in_=inb.ap()), reads=["inb"], writes=["outb"],
                 dma=True)
        sample_pass()
        S.barrier()
        A.reset(M1)
        prompt_pass(True)

    wpref = []
    if stage >= 3:
        if stage >= 4:
            wv_ = wmo.rearrange("(k p) n -> p k n", p=128)
            for bb in range(WSLOTS):
                wpref.append(wload(wv_[:, :, bb * 256:(bb + 1) * 256], 16, 256))
        scan_phase()
        S.barrier()
        if stage == 3:
            dbg_of = dout("dbg_of", [128, DC, NT], BF16)
            S.op("sp", lambda e: e.dma_start(out=dbg_of, in_=ofin), dma=True)
            S.barrier()
    A.reset(RX)

    def outproj():
        wv = wmo.rearrange("(k p) n -> p k n", p=128)
        for bb in range(8):
            sl, wb = wpref[bb] if bb < len(wpref) else wload(wv[:, :, bb * 256:(bb + 1) * 256], 16, 256)
            for q in range(2):
                d = 2 * bb + q
                ui, ures = next_unit()
                lin_job(ui, ures, [wb[:, k, q * 128:(q + 1) * 128] for k in range(DC)],
                        [ofin[:, k, :] for k in range(DC)], [("ofin", c) for c in range(DC)] + [("w", sl)])
                S.op("dve", lambda e, ui=ui, d=d: e.tensor_tensor(out=h[:, d, :], in0=unit_flat(ui), in1=h[:, d, :],
                                                                  op=ALU.add),
                     writes=ures + [("h", d)])

    if stage >= 4:
        outproj()
        S.barrier()
        hid = [A.alloc([128, 4, NT], BF16) for _ in range(2)]
        ffn(2, w2i, w2o, hid)
        S.barrier()
        A.reset(RX)

    rmsnorm(3, h, "h")
    yst = [A.alloc([128, D], F32) for _ in range(2)]
    for ti, (t0, n) in enumerate(ttiles):
        yb = yst[ti % 2]
        for g in range(4):
            bank = 6 + (g % 2)

            def f_tr(e, t0=t0, n=n, g=g, bank=bank):
                ins = None
                for q in range(4):
                    c = 4 * g + q
                    ins = e.transpose(out=psum[0:n, bank, q * 128:(q + 1) * 128], in_=h[:, c, t0:t0 + n],
                                      identity=ident)
                return ins
            S.op("pe", f_tr, reads=[("h", c) for c in range(4 * g, 4 * g + 4)] + ["cs"], writes=[("bank", bank)])
            S.op("act", lambda e, yb=yb, n=n, g=g, bank=bank: e.activation(
                out=yb[0:n, g * 512:(g + 1) * 512], in_=psum[0:n, bank, :], func=AF.Copy),
                writes=[("bank", bank), ("yst", ti % 2, g)])
        S.op("sp", lambda e, yb=yb, t0=t0, n=n: e.dma_start(out=y[t0:t0 + n, :], in_=yb[0:n, :]),
             reads=[("yst", ti % 2, g) for g in range(4)], dma=True)

    block = es.enter_context(nc.Block())
    S.emit(nc, block, es)
    es.close()
    return nc


def make_consts(flag=0.0):
    c = np.zeros((128, 672), np.float32)
    c[:, 0:128] = np.eye(128, dtype=np.float32)
    idx = np.arange(128)
    for (o, blk) in ((128, 64), (384, 8)):
        same = (idx[:, None] // blk) == (idx[None, :] // blk)
        c[:, o:o + 128] = (same & (idx[:, None] <= idx[None, :])).astype(np.float32)
        c[:, o + 128:o + 256] = (same & (idx[:, None] > idx[None, :])).astype(np.float32)
    c[:, 640:656] = ((idx[:, None] // 8) == np.arange(16)[None, :]).astype(np.float32)
    c[:, 656] = flag
    return c


_CACHE = {}


def kernel(x_prompt, x_sample, state_hgrn, state_gla, meta_tokens, lb_logits,
           ffn1_norm, w_ffn1_in, w_ffn1_out, mix_norm, w_in, w_alpha_up, b_alpha, gnorm_a, gnorm_b, w_out,
           ffn2_norm, w_ffn2_in, w_ffn2_out, final_norm):
    f = lambda a: np.ascontiguousarray(np.asarray(a, dtype=np.float32))
    x_prompt, x_sample, state_hgrn, state_gla = f(x_prompt), f(x_sample), f(state_hgrn), f(state_gla)
    meta = f(meta_tokens)
    if "nc" not in _CACHE:
        _CACHE["nc"] = build_program()
    nc = _CACHE["nc"]
    shared = dict(
        nrm=f(np.stack([f(ffn1_norm)[0], f(mix_norm)[0], f(ffn2_norm)[0], f(final_norm)])),
        w1i=f(w_ffn1_in)[0], w1o=f(w_ffn1_out)[0], w2i=f(w_ffn2_in)[0], w2o=f(w_ffn2_out)[0],
        wmi=f(w_in)[0], wal=f(np.concatenate([f(w_alpha_up)[0], f(b_alpha)[0][None, :]], axis=0)),
        lbl=f(lb_logits), gn=f(np.stack([f(gnorm_a)[0], f(gnorm_b)[0]])), wmo=f(w_out)[0])
    half = 1016
    in_maps = []
    for c in range(8):
        b, odd = c // 2, c % 2
        xs = x_sample[16 * c:16 * c + 16].reshape(128, D)
        if odd:
            xp = x_prompt[b, half:]
        else:
            xp = np.concatenate([meta, x_prompt[b, :half]], axis=0)
        m = dict(shared)
        m["xin"] = f(np.concatenate([xs, xp], axis=0))
        m["cst"] = make_consts(float(odd))
        m["stA"] = f(state_hgrn[0, 16 * c:16 * c + 16])
        m["stB"] = f(state_gla[0, 16 * c:16 * c + 16])
        in_maps.append(m)
    res = run_bass_kernel_spmd(nc, in_maps, core_ids=list(range(8)))
    R = res.results
    y_prompt = np.empty((4, 2048, D), np.float32)
    y_sample = np.empty((128, 8, D), np.float32)
    na_p = np.empty((1, 4, 8, 128, 128), np.float32)
    nb_p = np.empty((1, 4, 4, 128, 256), np.float32)
    na_s = np.empty((1, 128, 8, 128, 128), np.float32)
    nb_s = np.empty((1, 128, 4, 128, 256), np.float32)
    for c in range(8):
        b, odd = c // 2, c % 2
        yk = np.asarray(R[c]["y"], np.float32)
        y_sample[16 * c:16 * c + 16] = yk[0:128].reshape(16, 8, D)
        if odd:
            y_prompt[b, half:] = yk[128:]
            na_p[0, b] = np.asarray(R[c]["npA"], np.float32)
            nb_p[0, b] = np.asarray(R[c]["npB"], np.float32)
        else:
            y_prompt[b, :half] = yk[128 + N_META:]
        na_s[0, 16 * c:16 * c + 16] = np.asarray(R[c]["nsA"], np.float32)
        nb_s[0, 16 * c:16 * c + 16] = np.asarray(R[c]["nsB"], np.float32)
    return (y_prompt, y_sample, na_p, nb_p, na_s, nb_s)
```

```python
import os
import numpy as np
from contextlib import ExitStack
import concourse.bass as bass
import concourse.mybir as mybir
from concourse.bass_utils import run_bass_kernel_spmd

F32 = mybir.dt.float32
BF16 = mybir.dt.bfloat16
AF = mybir.ActivationFunctionType
ALU = mybir.AluOpType

D = 2048
DC = 16
NT = 1160
NP = 1032
DFF = 5632
FC = 44
N_META = 16
EPS = 1e-6
TT = [(0, 512), (512, 512), (1024, 136)]
STILES = [(0, 128, "S"), (128, 16, "P")] + [(144 + 128 * i, 128, "P") for i in range(7)] + [(1040, 120, "P")]
WSLOTS = 4
SKIP = set(os.environ.get('KSKIP', '').split(','))
WBLK = 4096
NDMASEM = 16


class Op:
    __slots__ = ("idx", "eng", "fn", "deps", "dma", "need_inc", "ticket", "nobar", "prevsame", "special")

    def __init__(self, idx, eng, fn, dma, nobar):
        self.idx = idx
        self.eng = eng
        self.fn = fn
        self.dma = dma
        self.deps = set()
        self.need_inc = False
        self.ticket = None
        self.nobar = nobar
        self.prevsame = None
        self.special = False


class Sched:
    ENGS = ("pe", "act", "dve", "pool", "sp")

    def __init__(self):
        self.ops = []
        self.last_w = {}
        self.readers = {}
        self.last_on_eng = {e: None for e in self.ENGS}
        self.pending_bar = {e: set() for e in self.ENGS}
        self.dma_since_bar = []

    def op(self, eng, fn, reads=(), writes=(), dma=False, nobar=False, special=False):
        o = Op(len(self.ops), eng, fn, dma or special, nobar)
        o.special = special
        deps = o.deps
        for r in reads:
            w = self.last_w.get(r)
            if w is not None:
                deps.add(w)
        for r in writes:
            w = self.last_w.get(r)
            if w is not None:
                deps.add(w)
            rd = self.readers.get(r)
            if rd:
                for k, v in rd.items():
                    if k == "dma":
                        deps.update(v)
                    else:
                        deps.add(v)
        if self.pending_bar[eng] and not nobar:
            deps.update(self.pending_bar[eng])
            self.pending_bar[eng] = set()
        for r in reads:
            rd = self.readers.setdefault(r, {})
            if dma:
                rd.setdefault("dma", []).append(o.idx)
            else:
                rd[eng] = o.idx
        for r in writes:
            self.last_w[r] = o.idx
            self.readers[r] = {}
        deps.discard(o.idx)
        self.ops.append(o)
        if not dma:
            self.last_on_eng[eng] = o.idx
        elif not nobar:
            self.dma_since_bar.append(o.idx)
        return o

    def barrier(self):
        s = set(self.dma_since_bar)
        for e in self.ENGS:
            if self.last_on_eng[e] is not None:
                s.add(self.last_on_eng[e])
        self.dma_since_bar = []
        for e in self.ENGS:
            self.pending_bar[e] = set(s) | self.pending_bar[e]

    def emit(self, nc, block, es):
        ops = self.ops
        for o in ops:
            best = {}
            keep = set()
            for d in o.deps:
                p = ops[d]
                if p.dma:
                    keep.add(d)
                else:
                    if p.eng == o.eng and not o.dma and o.eng == "pe":
                        continue
                    if p.eng not in best or best[p.eng] < d:
                        best[p.eng] = d
            keep.update(best.values())
            o.deps = keep
            for d in keep:
                ops[d].need_inc = True
        lasts = {e: self.last_on_eng[e] for e in self.ENGS}
        for e in self.ENGS:
            if lasts[e] is not None:
                ops[lasts[e]].need_inc = True
        sems = {e: es.enter_context(nc.semaphore("s_" + e)) for e in self.ENGS}
        qbase = {}
        nsem = 0
        for e in self.ENGS:
            qbase[e] = nsem
            nsem += NDMASEM
        dsems = [es.enter_context(nc.semaphore("d%d" % i)) for i in range(nsem)]
        cnt = {e: 0 for e in self.ENGS}
        dcnt = [0] * nsem
        dlast = [None] * nsem
        nd = {e: 0 for e in self.ENGS}
        for o in ops:
            if o.special:
                sp_sem = es.enter_context(nc.semaphore("cc%d" % o.idx))
                dsems.append(sp_sem)
                o.ticket = (len(dsems) - 1, 1, None)
            elif o.dma:
                k = qbase[o.eng] + nd[o.eng] % NDMASEM
                nd[o.eng] += 1
                dcnt[k] += 16
                o.ticket = (k, dcnt[k], 16)
                o.prevsame = dlast[k]
                dlast[k] = o.idx
            elif o.need_inc:
                cnt[o.eng] += 1
                o.ticket = (o.eng, cnt[o.eng], 1)
        per_eng = {e: [o for o in ops if o.eng == e] for e in self.ENGS}
        all_dma = [o for o in ops if o.dma]
        self.stats = {e: len(per_eng[e]) for e in self.ENGS}

        def semof(key):
            return sems[key] if isinstance(key, str) else dsems[key]

        def run(ename, eng):
            known = {}

            def wait(t):
                key, val, _ = t
                if known.get(key, 0) >= val:
                    return
                known[key] = val
                eng.wait_ge(semof(key), val)

            for o in per_eng[ename]:
                for d in sorted(o.deps):
                    wait(ops[d].ticket)
                if o.dma and o.prevsame is not None:
                    wait(ops[o.prevsame].ticket)
                ins = o.fn(eng)
                if o.ticket is not None:
                    if o.ticket[2] is None:
                        ins.then_inc(semof(o.ticket[0]))
                    else:
                        ins.then_inc(semof(o.ticket[0]), o.ticket[2])
            if ename == "sp":
                for o in all_dma:
                    wait(o.ticket)
                for e in self.ENGS:
                    if lasts[e] is not None:
                        wait(ops[lasts[e]].ticket)

        block.tensor(lambda e: run("pe", e))
        block.scalar(lambda e: run("act", e))
        block.vector(lambda e: run("dve", e))
        block.gpsimd(lambda e: run("pool", e))
        block.sync(lambda e: run("sp", e))


class Arena:
    def __init__(self, t, nbytes, base4=0):
        self.t = t
        self.n4 = base4 + nbytes // 4
        self.off = base4

    def alloc(self, shape, dtype, at=None):
        free = int(np.prod(shape[1:]))
        esz = 4 if dtype == F32 else 2
        n4 = (free * esz + 3) // 4
        n4 = (n4 + 7) // 8 * 8
        if at is None:
            at = self.off
            self.off += n4
            assert self.off <= self.n4, ("arena overflow", self.off * 4)
        v = self.t[:, at:at + n4]
        if dtype != F32:
            v = v.bitcast(dtype)
        v = v[:, 0:free]
        if len(shape) == 3:
            v = v.rearrange("p (a b) -> p a b", a=shape[1])
        elif len(shape) == 4:
            v = v.rearrange("p (a b c) -> p a b c", a=shape[1], b=shape[2])
        if shape[0] < 128:
            v = v[0:shape[0]]
        return v

    def mark(self):
        return self.off

    def reset(self, m):
        self.off = m


def build_program(stage=99, use_cc=True):
    nc = bass.Bass("TRN2", target_bir_lowering=False)
    S = Sched()
    es = ExitStack()

    def din(name, shape, dt=F32):
        return nc.dram_tensor(name, list(shape), dt, kind="ExternalInput").ap()

    def dout(name, shape, dt=F32):
        return nc.dram_tensor(name, list(shape), dt, kind="ExternalOutput").ap()

    xin = din("xin", [NT, D])
    cst = din("cst", [128, 672])
    nrm = din("nrm", [4, D])
    w1i = din("w1i", [D, 2 * DFF])
    w1o = din("w1o", [DFF, D])
    w2i = din("w2i", [D, 2 * DFF])
    w2o = din("w2o", [DFF, D])
    y = dout("y", [NT, D])
    wmi = din("wmi", [D, 7184])
    wal_d = din("wal", [17, 512])
    lbl = din("lbl", [2, 1024])
    gn_d = din("gn", [2, 1024])
    wmo = din("wmo", [D, D])
    stA = din("stA", [16, 8, 128, 128])
    stB = din("stB", [16, 4, 128, 256])
    nsA = dout("nsA", [16, 8, 128, 128])
    nsB = dout("nsB", [16, 4, 128, 256])
    npA = dout("npA", [8, 128, 128])
    npB = dout("npB", [4, 128, 256])
    inb = nc.dram_tensor("inb", [128, 2048], F32)
    outb = nc.dram_tensor("outb", [256, 2048], F32)

    def scr(name, shape, dt):
        if stage < 99:
            return nc.dram_tensor(name, list(shape), dt, kind="ExternalOutput").ap()
        return nc.dram_tensor(name, list(shape), dt).ap()

    FM = scr("FM", [40, 128, NT], BF16)
    lgA = scr("lgA", [2, NT, 1024], BF16)
    kA = scr("kA", [NT, 1024], BF16)
    vA = scr("vA", [NT, 1024], BF16)
    lgB = scr("lgB", [2, NT, 512], BF16)
    kB = scr("kB", [NT, 512], BF16)
    vB = scr("vB", [NT, 1024], BF16)

    arena_t = es.enter_context(nc.sbuf_tensor("arena", [128, 53000], F32))
    A = Arena(arena_t, 212000)
    psum = es.enter_context(nc.psum_tensor("ps", [128, 8, 512], F32))

    h = A.alloc([128, DC, NT], F32)
    u = A.alloc([128, DC, NT], BF16)
    wring = [A.alloc([128, WBLK], BF16) for _ in range(WSLOTS)]
    cs = A.alloc([128, 672], F32)
    nw = A.alloc([128, 4, DC], F32)
    VOFF = A.mark()
    vecf = [A.alloc([128, NT], F32) for _ in range(2)]
    stg = [A.alloc([128, NT], BF16) for _ in range(3)]
    RX = A.mark()

    ident = cs[:, 0:128]
    ones_bf = A.alloc([128, 128], BF16)
    eps_t = A.alloc([128, 1], F32)
    RX = A.mark()

    pe_unit = [psum[:, 0:3, :], psum[:, 3:6, :]]

    def unit_flat(ui):
        return psum[:, 3 * ui:3 * ui + 3, :].rearrange("p a b -> p (a b)")[:, 0:NT]

    state = {"unit": 0, "stg": 0, "vec": 0}

    S.op("sp", lambda e: e.dma_start(out=cs, in_=cst), writes=["cs"], dma=True)
    with nc.allow_non_contiguous_dma(reason="small param load"):
        pass
    S.op("act", lambda e: e.dma_start(out=nw, in_=nrm.rearrange("n (c p) -> p n c", p=128),
                                      allow_slow_non_contiguous=True), writes=["nw"], dma=True)
    S.op("dve", lambda e: e.memset(ones_bf, 1.0), writes=["ones"])
    S.op("dve", lambda e: e.memset(eps_t, EPS), writes=["eps"])

    xst = [A.alloc([128, D], F32) for _ in range(2)]
    ttiles = [(i * 128, 128) for i in range(9)] + [(1152, 8)]
    for ti, (t0, n) in enumerate(ttiles):
        xb = xst[ti % 2]
        S.op("sp", (lambda e, xb=xb, t0=t0, n=n: e.dma_start(out=xb[0:n, :], in_=xin[t0:t0 + n, :])),
             writes=[("xst", ti % 2)], dma=True)
        for g in range(4):
            bank = 6 + (g % 2)

            def f_tr(e, xb=xb, n=n, g=g, bank=bank):
                ins = None
                for q in range(4):
                    c = 4 * g + q
                    ins = e.transpose(out=psum[:, bank, q * 128:q * 128 + n], in_=xb[0:n, c * 128:(c + 1) * 128],
                                      identity=ident[0:n, 0:n])
                return ins
            S.op("pe", f_tr, reads=[("xst", ti % 2), "cs"], writes=[("bank", bank)])

            def f_cp(e, g=g, bank=bank, t0=t0, n=n):
                src = psum[:, bank, :].rearrange("p (q t) -> p q t", q=4)[:, :, 0:n]
                return e.activation(out=h[:, 4 * g:4 * g + 4, t0:t0 + n], in_=src, func=AF.Copy)
            S.op("act", f_cp, writes=[("bank", bank)] + [("h", c) for c in range(4 * g, 4 * g + 4)])
    S.barrier()
    A.reset(RX)

    def rmsnorm(widx, dst, dst_res):
        ui = state["unit"]
        state["unit"] ^= 1
        ures = [("bank", 3 * ui + k) for k in range(3)]
        for c in range(DC):
            sb = stg[c % 3]
            if c % 2 == 0:
                S.op("act", lambda e, c=c, sb=sb: e.activation(out=sb, in_=h[:, c, :], func=AF.Square),
                     reads=[("h", c)], writes=[("stg", c % 3)])
            else:
                S.op("dve", lambda e, c=c, sb=sb: e.tensor_tensor(out=sb, in0=h[:, c, :], in1=h[:, c, :], op=ALU.mult),
                     reads=[("h", c)], writes=[("stg", c % 3)])

            def f_mm(e, c=c, sb=sb, ui=ui):
                ins = None
                for k, (t0, n) in enumerate(TT):
                    ins = e.matmul(psum[:, 3 * ui + k, 0:n], lhsT=ones_bf, rhs=sb[:, t0:t0 + n],
                                   start=(c == 0), stop=(c == DC - 1))
                return ins
            S.op("pe", f_mm, reads=[("stg", c % 3), "ones"], writes=ures)
        rt, rinv = vecf
        S.op("act", lambda e, ui=ui: e.activation(out=rt, in_=unit_flat(ui), func=AF.Ln, bias=eps_t[:, 0:1],
                                                  scale=1.0 / D),
             reads=["eps"], writes=ures + [("vec", 0)])
        S.op("act", lambda e: e.activation(out=rinv, in_=rt, func=AF.Exp, scale=-0.5),
             reads=[("vec", 0)], writes=[("vec", 1)])
        for c in range(DC):
            S.op("dve", lambda e, c=c: e.scalar_tensor_tensor(out=dst[:, c, :], in0=h[:, c, :],
                                                               scalar=nw[:, widx, c:c + 1], in1=rinv,
                                                               op0=ALU.mult, op1=ALU.mult),
                 reads=[("h", c), ("vec", 1), "nw"], writes=[(dst_res, c)])

    wstate = {"n": 0}

    def wload(src, a, b):
        slot = wstate["n"] % WSLOTS
        wstate["n"] += 1
        dst = wring[slot][:, 0:a * b].rearrange("p (a b) -> p a b", a=a)
        S.op("pool", lambda e, dst=dst, src=src: e.dma_start(out=dst, in_=src), writes=[("w", slot)],
             dma=True, nobar=True)
        return slot, dst

    def next_unit():
        ui = state["unit"]
        state["unit"] ^= 1
        return ui, [("bank", 3 * ui + k) for k in range(3)]

    def lin_job(ui, ures, wtiles, acts, reads):
        nk = len(wtiles)

        def f(e):
            ins = None
            for k in range(nk):
                for b, (t0, n) in enumerate(TT):
                    ins = e.matmul(psum[0:wtiles[k].shape[1], 3 * ui + b, 0:n], lhsT=wtiles[k],
                                   rhs=acts[k][:, t0:t0 + n], start=(k == 0), stop=(k == nk - 1))
            return ins
        S.op("pe", f, reads=reads, writes=ures)

    U_ALL = [("u", c) for c in range(DC)]

    def ffn(widx, w_in, w_out, hid):
        rmsnorm(widx, u, "u")
        w_in_v = w_in.rearrange("(k p) n -> p k n", p=128)
        w_out_v = w_out.rearrange("(f p) n -> p f n", p=128)
        NG = FC // 4

        def up_group(g):
            hb = hid[g % 2]
            for pr in range(2):
                j0 = 4 * g + 2 * pr
                sl_g, wg = wload(w_in_v[:, :, j0 * 128:j0 * 128 + 256], 16, 256)
                sl_u, wu = wload(w_in_v[:, :, DFF + j0 * 128:DFF + j0 * 128 + 256], 16, 256)
                for q in range(2):
                    jj = 2 * pr + q
                    ui, ures = next_unit()
                    lin_job(ui, ures, [wg[:, k, q * 128:(q + 1) * 128] for k in range(DC)],
                            [u[:, k, :] for k in range(DC)], U_ALL + [("w", sl_g)])
                    vi = state["vec"]
                    state["vec"] ^= 1
                    sgb = vecf[vi]
                    S.op("act", lambda e, ui=ui, sgb=sgb: e.activation(out=sgb, in_=unit_flat(ui), func=AF.Silu),
                         writes=ures + [("vec", vi)])
                    ui2, ures2 = next_unit()
                    lin_job(ui2, ures2, [wu[:, k, q * 128:(q + 1) * 128] for k in range(DC)],
                            [u[:, k, :] for k in range(DC)], U_ALL + [("w", sl_u)])
                    S.op("dve", lambda e, ui2=ui2, sgb=sgb, hb=hb, jj=jj: e.tensor_tensor(
                        out=hb[:, jj, :], in0=unit_flat(ui2), in1=sgb, op=ALU.mult),
                        reads=[("vec", vi)], writes=ures2 + [("hid", g % 2, jj)])

        def down_group(g):
            hb = hid[g % 2]
            for half in range(2):
                sl, wd = wload(w_out_v[:, 4 * g:4 * g + 4, half * 1024:(half + 1) * 1024], 4, 1024)
                for dd in range(8):
                    d = half * 8 + dd
                    ui, ures = next_unit()
                    lin_job(ui, ures, [wd[:, f, dd * 128:(dd + 1) * 128] for f in range(4)],
                            [hb[:, f, :] for f in range(4)], [("hid", g % 2, f) for f in range(4)] + [("w", sl)])
                    S.op("dve", lambda e, ui=ui, d=d: e.scalar_tensor_tensor(
                        out=h[:, d, :], in0=unit_flat(ui), scalar=0.5, in1=h[:, d, :], op0=ALU.mult, op1=ALU.add),
                        writes=ures + [("h", d)])

        up_group(0)
        for g in range(NG):
            if g + 1 < NG:
                up_group(g + 1)
            down_group(g)

    hid = [A.alloc([128, 4, NT], BF16) for _ in range(2)]
    if stage >= 1:
        ffn(0, w1i, w1o, hid)
        S.barrier()
    A.reset(RX)

    def fm_idx(kind, i):
        if kind == "qa":
            return (i // 4) * 12 + i % 4
        if kind == "ka":
            return (i // 4) * 12 + 4 + i % 4
        if kind == "ga":
            return (i // 4) * 12 + 8 + i % 4
        if kind == "qb":
            return 24 + i
        if kind == "kb":
            return 28 + i
        return 32 + i

    def inproj():
        rmsnorm(1, u, "u")
        lb_bc = A.alloc([128, 1024], F32)
        oml_bc = A.alloc([128, 1024], F32)
        lbF = A.alloc([128, 2, 8], F32)
        lbf = A.alloc([128, 8], F32)
        omlf = A.alloc([128, 8], F32)
        tmf = [A.alloc([128, 256], F32) for _ in range(4)]
        tms = [A.alloc([128, 256], BF16) for _ in range(4)]
        aT = A.alloc([17, NT], F32)
        wal = A.alloc([17, 512], F32)
        lgs = [A.alloc([128, 512], F32) for _ in range(4)]
        thl = [A.alloc([128, 2, 256], BF16) for _ in range(4)]
        tmf2 = [A.alloc([128, 256], F32) for _ in range(4)]
        lgsb = [A.alloc([128, 2, 512], BF16) for _ in range(4)]
        S.op("sp", lambda e: e.dma_start(out=lb_bc, in_=lbl[0:1, :].partition_broadcast(128)),
             writes=["lb_bc"], dma=True)
        S.op("sp", lambda e: e.dma_start(out=oml_bc, in_=lbl[1:2, :].partition_broadcast(128)),
             writes=["oml_bc"], dma=True)
        S.op("sp", lambda e: e.dma_start(out=lbF, in_=lbl.rearrange("n (c p) -> p n c", p=128),
                                         allow_slow_non_contiguous=True), writes=["lbF"], dma=True)
        S.op("sp", lambda e: e.dma_start(out=wal, in_=wal_d), writes=["wal"], dma=True)
        S.op("dve", lambda e: e.tensor_tensor(out=lb_bc, in0=lb_bc, in1=oml_bc, op=ALU.subtract),
             reads=["oml_bc"], writes=["lb_bc"])
        S.op("act", lambda e: e.activation(out=oml_bc, in_=lb_bc, func=AF.Sigmoid, scale=-1.0),
             reads=["lb_bc"], writes=["oml_bc"])
        S.op("act", lambda e: e.activation(out=lb_bc, in_=lb_bc, func=AF.Sigmoid),
             reads=["oml_bc"], writes=["lb_bc"])
        S.op("dve", lambda e: e.tensor_tensor(out=lbf, in0=lbF[:, 0, :], in1=lbF[:, 1, :], op=ALU.subtract),
             reads=["lbF"], writes=["lbf"])
        S.op("act", lambda e: e.activation(out=omlf, in_=lbf, func=AF.Sigmoid, scale=-1.0),
             reads=["lbf"], writes=["omlf"])
        S.op("dve", lambda e: e.memset(aT, 1.0), writes=["aT"])

        w_v = wmi.rearrange("(k p) n -> p k n", p=128)
        st = {"stg": 0, "tm": 0, "bq": 0}
        TMBANKS = [6, 7, 2, 5]

        def fm_job(wb, q, kind, i, sl):
            ui, ures = next_unit()
            lin_job(ui, ures, [wb[:, k, q * 128:(q + 1) * 128] for k in range(DC)],
                    [u[:, k, :] for k in range(DC)], U_ALL + [("w", sl)])
            si = st["stg"] % 3
            st["stg"] += 1
            sb = stg[si]
            if kind == "fa":
                vi = state["vec"]
                state["vec"] ^= 1
                vb_ = vecf[vi]
                S.op("act", lambda e: e.activation(out=vb_, in_=unit_flat(ui), func=AF.Sigmoid, scale=-1.0),
                     writes=ures + [("vec", vi)])
                S.op("dve", lambda e: e.tensor_scalar(out=sb, in0=vb_, scalar1=omlf[:, i:i + 1], scalar2=None,
                                                      op0=ALU.mult),
                     reads=[("vec", vi), "omlf"], writes=[("stg", si)])
                idx = fm_idx("ka", i)
            else:
                func = AF.Silu if kind in ("qa", "ga", "rb") else AF.Copy
                S.op("act", lambda e: e.activation(out=sb, in_=unit_flat(ui), func=func),
                     writes=ures + [("stg", si)])
                idx = fm_idx(kind, i)
            S.op("sp", lambda e: e.dma_start(out=FM[idx], in_=sb), reads=[("stg", si)], writes=[("FM", idx)],
                 dma=True)

        def tm_jobs(wb, kind, c0, sl):
            if "tmfa" in SKIP and kind == "fa":
                return
            if "tmother" in SKIP and kind != "fa":
                return
            def pe_job(t0, n):
                bq = TMBANKS[st["bq"] % len(TMBANKS)]
                st["bq"] += 1
                pv = psum[0:n, bq, 0:256]

                def f(e):
                    ins = None
                    for k in range(DC):
                        ins = e.matmul(pv, lhsT=u[:, k, t0:t0 + n], rhs=wb[:, k, :], start=(k == 0),
                                       stop=(k == DC - 1))
                    return ins
                S.op("pe", f, reads=U_ALL + [("w", sl)], writes=[("bank", bq)])
                ti = st["tm"] % 4
                st["tm"] += 1
                return bq, pv, ti

            def fa_first(t0, n):
                bq, pv, ti = pe_job(t0, n)
                a_ = tmf[ti]
                ra = ("tmf", ti)
                S.op("act", lambda e: e.activation(out=a_[0:n], in_=pv, func=AF.Sigmoid), writes=[("bank", bq), ra])
                S.op("dve", lambda e: e.tensor_tensor(out=a_[0:n], in0=a_[0:n], in1=oml_bc[0:n, c0:c0 + 256],
                                                      op=ALU.mult), reads=[ra, "oml_bc"], writes=[ra])
                S.op("dve", lambda e: e.tensor_tensor(out=a_[0:n], in0=a_[0:n], in1=lb_bc[0:n, c0:c0 + 256],
                                                      op=ALU.add), reads=[ra, "lb_bc"], writes=[ra])
                return ti

            def fa_second(t0, n, ti):
                a_, b_, kb_, hl = tmf[ti], tmf2[ti], tms[ti], thl[ti]
                ra, rb_ = ("tmf", ti), ("tmf2", ti)
                S.op("act", lambda e: e.activation(out=b_[0:n], in_=a_[0:n], func=AF.Ln), reads=[ra], writes=[rb_])
                S.op("dve", lambda e: e.tensor_copy(out=hl[0:n, 0, :], in_=b_[0:n]), reads=[rb_], writes=[("thl", ti)])
                S.op("dve", lambda e: e.tensor_tensor(out=hl[0:n, 1, :], in0=b_[0:n], in1=hl[0:n, 0, :],
                                                      op=ALU.subtract), reads=[rb_], writes=[("thl", ti)])
                S.op("sp", lambda e: e.dma_start(
                    out=lgA[:, t0:t0 + n, c0:c0 + 256].rearrange("a t c -> t a c"), in_=hl[0:n]),
                    reads=[("thl", ti)], dma=True)
                S.op("dve", lambda e: e.tensor_scalar(out=kb_[0:n], in0=a_[0:n], scalar1=-1.0, scalar2=1.0,
                                                      op0=ALU.mult, op1=ALU.add), reads=[ra], writes=[("tms", ti)])
                S.op("sp", lambda e: e.dma_start(out=kA[t0:t0 + n, c0:c0 + 256], in_=kb_[0:n]),
                     reads=[("tms", ti)], dma=True)

            def other(t0, n):
                bq, pv, ti = pe_job(t0, n)
                dst = {"ia": vA, "kb": kB, "vb": vB}[kind]
                sb = tms[ti]
                S.op("act", lambda e: e.activation(out=sb[0:n], in_=pv, func=AF.Copy),
                     writes=[("bank", bq), ("tms", ti)])
                S.op("sp", lambda e: e.dma_start(out=dst[t0:t0 + n, c0:c0 + 256], in_=sb[0:n]),
                     reads=[("tms", ti)], dma=True)

            tl = [(t0, n) for (t0, n, _) in STILES]
            if kind == "fa":
                for c_ in range(0, len(tl), 4):
                    chunk = tl[c_:c_ + 4]
                    tis = [fa_first(t0, n) for (t0, n) in chunk]
                    for (t0, n), ti in zip(chunk, tis):
                        fa_second(t0, n, ti)
            else:
                for (t0, n) in tl:
                    other(t0, n)

        plan = [("qa", 0, 4, True, False), ("fa", 1024, 4, True, True), ("ia", 2048, 4, False, True),
                ("ga", 3072, 4, True, False), ("qb", 4096, 2, True, False), ("kb", 4608, 2, True, True),
                ("vb", 5120, 4, False, True), ("rb", 6144, 4, True, False)]
        for kind, col0, nb, do_fm, do_tm in plan:
            for b in range(nb):
                sl, wb = wload(w_v[:, :, col0 + b * 256:col0 + (b + 1) * 256], 16, 256)
                if do_fm:
                    for q in range(2):
                        fm_job(wb, q, kind, 2 * b + q, sl)
                if do_tm and "tm" not in SKIP:
                    tm_jobs(wb, kind, b * 256, sl)
        if "alow" in SKIP:
            return
        sl, wb = wload(w_v[:, :, 7168:7184], 16, 16)
        ui, ures = next_unit()
        lin_job(ui, ures, [wb[:, k, :] for k in range(DC)], [u[:, k, :] for k in range(DC)], U_ALL + [("w", sl)])
        S.op("act", lambda e: e.activation(out=aT[0:16, :], in_=unit_flat(ui)[0:16, :], func=AF.Copy),
             writes=ures + ["aT"])
        def lgb_first(j, t0, n):
            bank = TMBANKS[j]
            S.op("pe", lambda e: e.matmul(psum[0:n, bank, :], lhsT=aT[:, t0:t0 + n], rhs=wal, start=True, stop=True),
                 reads=["aT", "wal"], writes=[("bank", bank)])
            lb_ = lgs[j]
            S.op("act", lambda e: e.activation(out=lb_[0:n], in_=psum[0:n, bank, :], func=AF.Sigmoid),
                 writes=[("bank", bank), ("lgs", j)])

        def lgb_second(j, t0, n):
            lb_, hb_ = lgs[j], lgsb[j]
            S.op("act", lambda e: e.activation(out=lb_[0:n], in_=lb_[0:n], func=AF.Ln),
                 reads=[("lgs", j)], writes=[("lgs", j)])
            S.op("dve", lambda e: e.tensor_scalar(out=lb_[0:n], in0=lb_[0:n], scalar1=1.0 / 16.0, scalar2=None,
                                                  op0=ALU.mult), reads=[("lgs", j)], writes=[("lgs", j)])
            S.op("dve", lambda e: e.tensor_copy(out=hb_[0:n, 0, :], in_=lb_[0:n]),
                 reads=[("lgs", j)], writes=[("lgsb", j)])
            S.op("dve", lambda e: e.tensor_tensor(out=hb_[0:n, 1, :], in0=lb_[0:n], in1=hb_[0:n, 0, :],
                                                  op=ALU.subtract), reads=[("lgs", j)], writes=[("lgsb", j)])
            S.op("sp", lambda e: e.dma_start(out=lgB[:, t0:t0 + n, :].rearrange("a t c -> t a c"), in_=hb_[0:n]),
                 reads=[("lgsb", j)], dma=True)

        if "lgb" not in SKIP:
            tl = [(t0, n) for (t0, n, _) in STILES]
            for c_ in range(0, len(tl), 4):
                chunk = tl[c_:c_ + 4]
                for j, (t0, n) in enumerate(chunk):
                    lgb_first(j, t0, n)
                for j, (t0, n) in enumerate(chunk):
                    lgb_second(j, t0, n)

    if stage >= 2:
        inproj()
        S.barrier()
    A.reset(RX)

    ofin = u
    LP, UP, LSm, USm = "LP", "UP", "LS", "US"
    mkf = {"LP": cs[:, 128:256], "UP": cs[:, 256:384], "LS": cs[:, 384:512], "US": cs[:, 512:640]}
    mkb = {}
    BM = cs[:, 640:656]
    flag_t = cs[:, 656:657]
    GROUPS = [dict(g=0, V=128, lg=lgA, lc=0, k=kA, v=vA, vc=0, fmb=0, nfm=12),
              dict(g=1, V=128, lg=lgA, lc=512, k=kA, v=vA, vc=512, fmb=12, nfm=12),
              dict(g=2, V=256, lg=lgB, lc=0, k=kB, v=vB, vc=0, fmb=24, nfm=16)]

    def scan_phase():
        A2 = Arena(arena_t, 16240, VOFF)
        gnF = A.alloc([128, 2, 8], F32)
        S.op("sp", lambda e: e.dma_start(out=gnF, in_=gn_d.rearrange("n (c p) -> p n c", p=128),
                                         allow_slow_non_contiguous=True), writes=["gnF"], dma=True)
        Sst = A.alloc([128, 4, 256], F32)
        Sbf = A.alloc([128, 4, 256], BF16)
        NSLOT = 4
        work = []
        for sl in range(NSLOT):
            base = dict(eNB=A.alloc([128, 128], F32), eD=A.alloc([128, 128], F32), kt=A.alloc([128, 128], BF16),
                        sq=A.alloc([128, 2, 128], BF16), rt=A.alloc([128, 128], F32), tmp=A.alloc([128, 2, 128], F32))
            pars = []
            for par in range(2):
                d_ = dict(base)
                d_.update(eB=A.alloc([128, 128], F32), qt=A.alloc([128, 128], BF16), kdec=A.alloc([128, 128], BF16),
                          ATm=A.alloc([128, 128], BF16))
                pars.append(d_)
            work.append(pars)
        M0 = A.mark()
        cnt = {"ld": 0}

        def load_tile(G, t0, n, need_fm):
            b = cnt["ld"] % 2
            cnt["ld"] += 1
            V = G["V"]
            S.op("sp", lambda e: e.dma_start(
                out=lgt[b][0:n], in_=G["lg"][:, t0:t0 + n, G["lc"]:G["lc"] + 512].rearrange("a t c -> t a c")),
                writes=[("lgt", b)], dma=True)
            S.op("sp", lambda e: e.dma_start(out=ktm[b][0:n, :], in_=G["k"][t0:t0 + n, G["lc"]:G["lc"] + 512]),
                 writes=[("ktm", b)], dma=True)
            S.op("sp", lambda e: e.dma_start(out=vtm[b][0:n, 0:4 * V], in_=G["v"][t0:t0 + n, G["vc"]:G["vc"] + 4 * V]),
                 writes=[("vtm", b)], dma=True)
            if need_fm:
                fb = G["fmb"]
                for (c0, c1) in [(0, 4), (4, 8), (8, G["nfm"])]:
                    S.op("sp", lambda e, c0=c0, c1=c1: e.dma_start(
                        out=FMt[b][:, c0:c1, 0:n], in_=FM[fb + c0:fb + c1, :, t0:t0 + n].rearrange("c p t -> p c t")),
                        writes=[("FMt", b, c0)], dma=True)
            return b

        def head_core(G, i, b, n, sl, Lm, Um, main, par=0):
            W = work[sl][par]
            bA = ("bank", 2 * sl)
            lgh = lgt[b][0:n, 0, i * 128:(i + 1) * 128]
            lgl = lgt[b][0:n, 1, i * 128:(i + 1) * 128]
            ktok = ktm[b][0:n, i * 128:(i + 1) * 128]
            pA = 2 * sl
            Lmb, Umb = mkb[Lm], mkb[Um]

            def f1(e):
                e.matmul(psum[0:n, pA, 128:256], lhsT=Umb[0:n, 0:n], rhs=lgh, start=True, stop=False)
                e.matmul(psum[0:n, pA, 128:256], lhsT=Umb[0:n, 0:n], rhs=lgl, start=False, stop=True)
                e.matmul(psum[:, pA, 0:n], lhsT=lgh, rhs=Lmb[0:n, 0:n], start=True, stop=False)
                return e.matmul(psum[:, pA, 0:n], lhsT=lgl, rhs=Lmb[0:n, 0:n], start=False, stop=True)
            yield S.op("pe", f1, reads=[("lgt", b), "mkb"], writes=[bA])
            yield S.op("act", lambda e: e.activation(out=W["eD"][0:n, :], in_=psum[0:n, pA, 128:256], func=AF.Exp),
                 writes=[bA, ("eD", sl)])
            yield S.op("act", lambda e: e.activation(out=W["eB"][:, 0:n], in_=psum[:, pA, 0:n], func=AF.Exp),
                 writes=[bA, ("eB", sl, par)])
            yield S.op("pool" if main else "dve",
                       lambda e: e.tensor_tensor(out=W["kdec"][0:n, :], in0=ktok, in1=W["eD"][0:n, :], op=ALU.mult),
                 reads=[("ktm", b), ("eD", sl)], writes=[("kdec", sl, par)])
            if not main:
                return
            qc = i
            kc = 4 + i
            yield S.op("act", lambda e: e.activation(out=W["eNB"][:, 0:n], in_=psum[:, pA, 0:n], func=AF.Exp, scale=-1.0),
                 writes=[bA, ("eNB", sl)])
            yield S.op("dve", lambda e: e.scalar_tensor_tensor(out=W["qt"][:, 0:n], in0=FMt[b][:, qc, 0:n],
                                                          scalar=128.0 ** -0.5, in1=W["eB"][:, 0:n],
                                                          op0=ALU.mult, op1=ALU.mult),
                 reads=[("FMt", b, 0), ("eB", sl, par)], writes=[("qt", sl, par)])
            yield S.op("pool", lambda e: e.tensor_tensor(out=W["kt"][:, 0:n], in0=FMt[b][:, kc, 0:n], in1=W["eNB"][:, 0:n],
                                                   op=ALU.mult),
                 reads=[("FMt", b, 4), ("eNB", sl)], writes=[("kt", sl)])
            yield S.op("pe", lambda e: e.matmul(psum[0:n, pA, 256:256 + n], lhsT=W["kt"][:, 0:n], rhs=W["qt"][:, 0:n],
                                          start=True, stop=True),
                 reads=[("kt", sl), ("qt", sl, par)], writes=[bA])
            yield S.op("dve", lambda e: e.tensor_tensor(out=W["ATm"][0:n, 0:n], in0=psum[0:n, pA, 256:256 + n],
                                                   in1=mkf[Lm][0:n, 0:n], op=ALU.mult),
                 reads=["cs"], writes=[bA, ("ATm", sl, par)])

        def head_norm(G, i, b, t0, n, sl):
            W = work[sl][0]
            V = G["V"]
            nvh = V // 128
            pA, pB = 2 * sl, 2 * sl + 1
            bA, bB = ("bank", pA), ("bank", pB)
            ov = psum[:, pB, 0:256].rearrange("p (a t) -> p a t", a=2)
            yield S.op("act", lambda e: e.activation(out=W["sq"][:, 0:nvh, 0:n], in_=ov[:, 0:nvh, 0:n], func=AF.Square),
                 writes=[bB, ("sq", sl)])
            yield S.op("act", lambda e: e.activation(out=W["tmp"][:, 0:nvh, 0:n], in_=ov[:, 0:nvh, 0:n], func=AF.Copy),
                 writes=[bB] + [("tmp", sl, vh) for vh in range(nvh)])

            def fss(e):
                ins = None
                for vh in range(nvh):
                    ins = e.matmul(psum[:, pA, 384:384 + n], lhsT=ones_bf, rhs=W["sq"][:, vh, 0:n],
                                   start=(vh == 0), stop=(vh == nvh - 1))
                return ins
            yield S.op("pe", fss, reads=[("sq", sl), "ones"], writes=[bA])
            yield S.op("act", lambda e: e.activation(out=W["rt"][:, 0:n], in_=psum[:, pA, 384:384 + n], func=AF.Ln,
                                               bias=eps_t[:, 0:1], scale=1.0 / V),
                 reads=["eps"], writes=[bA, ("rt", sl)])
            yield S.op("act", lambda e: e.activation(out=W["rt"][:, 0:n], in_=W["rt"][:, 0:n], func=AF.Exp, scale=-0.5),
                       writes=[("rt", sl)])
            for vh in range(nvh):
                if G["g"] < 2:
                    gcol = gnF[:, 0, 4 * G["g"] + i:4 * G["g"] + i + 1]
                    mixc = 4 * G["g"] + i
                    gatec = 8 + i
                else:
                    gcol = gnF[:, 1, 2 * i + vh:2 * i + vh + 1]
                    mixc = 8 + 2 * i + vh
                    gatec = 8 + 2 * i + vh
                yield S.op("dve", lambda e, vh=vh, gcol=gcol: e.scalar_tensor_tensor(
                    out=W["tmp"][:, vh, 0:n], in0=W["tmp"][:, vh, 0:n], scalar=gcol, in1=W["rt"][:, 0:n],
                    op0=ALU.mult, op1=ALU.mult),
                    reads=[("rt", sl), "gnF"], writes=[("tmp", sl, vh)])
                yield S.op("pool", lambda e, vh=vh, mixc=mixc, gatec=gatec: e.tensor_tensor(
                    out=ofin[:, mixc, t0:t0 + n], in0=W["tmp"][:, vh, 0:n], in1=FMt[b][:, gatec, 0:n], op=ALU.mult),
                    reads=[("tmp", sl, vh), ("FMt", b, 8)], writes=[("ofin", mixc)])

        def prompt_block(G, i, b, sl, s0, nb, main, par=0):
            V = G["V"]
            nvh = V // 128
            W = work[sl][par]
            pB = 2 * sl + 1
            bB = ("bank", pB)
            vtok = vtm[b][:, i * V:(i + 1) * V]
            if main:
                def fo(e):
                    ins = None
                    for vh in range(nvh):
                        o_ = psum[:, pB, vh * 128 + s0:vh * 128 + s0 + nb]
                        e.matmul(o_, lhsT=Sbf[:, i, vh * 128:(vh + 1) * 128], rhs=W["qt"][:, s0:s0 + nb],
                                 start=True, stop=False)
                        ins = e.matmul(o_, lhsT=vtok[s0:s0 + nb, vh * 128:(vh + 1) * 128],
                                       rhs=W["ATm"][s0:s0 + nb, s0:s0 + nb], start=False, stop=True)
                    return ins
                yield S.op("pe", fo, reads=[("Sbf", i), ("qt", sl, par), ("vtm", b), ("ATm", sl, par)], writes=[bB])
            yield S.op("pe", lambda e: e.matmul(psum[:, pB, 256:256 + V], lhsT=W["kdec"][s0:s0 + nb, :],
                                          rhs=vtok[s0:s0 + nb, :], start=True, stop=True),
                 reads=[("kdec", sl, par), ("vtm", b)], writes=[bB])
            yield S.op("dve", lambda e: e.scalar_tensor_tensor(
                out=Sst[:, i, 0:V], in0=Sst[:, i, 0:V], scalar=W["eB"][:, s0 + nb - 1:s0 + nb],
                in1=psum[:, pB, 256:256 + V], op0=ALU.mult, op1=ALU.add),
                reads=[("eB", sl, par)], writes=[bB, ("Sst", i)])
            if main:
                yield S.op("dve", lambda e: e.tensor_copy(out=Sbf[:, i, 0:V], in_=Sst[:, i, 0:V]),
                     reads=[("Sst", i)], writes=[("Sbf", i)])

        def prepass_tile(G, i, b, sl, blocks, par):
            V = G["V"]
            W = work[sl][par]
            pB = 2 * sl + 1
            bB = ("bank", pB)
            vtok = vtm[b][:, i * V:(i + 1) * V]
            pA = 2 * sl
            bA = ("bank", pA)
            regs = [psum[:, pB, 256:256 + V], psum[:, pA, 256:256 + V]]
            rres = [[bB], [bA]]

            def fp(e):
                ins = None
                for bi, (s0, nb) in enumerate(blocks):
                    ins = e.matmul(regs[bi], lhsT=W["kdec"][s0:s0 + nb, :], rhs=vtok[s0:s0 + nb, :],
                                   start=True, stop=True)
                return ins
            yield S.op("pe", fp, reads=[("kdec", sl, par), ("vtm", b)], writes=[bB] + ([bA] if len(blocks) > 1 else []))
            for bi, (s0, nb) in enumerate(blocks):
                yield S.op("dve", lambda e, bi=bi, s0=s0, nb=nb: e.scalar_tensor_tensor(
                    out=Sst[:, i, 0:V], in0=Sst[:, i, 0:V], scalar=W["eB"][:, s0 + nb - 1:s0 + nb],
                    in1=regs[bi], op0=ALU.mult, op1=ALU.add),
                    reads=[("eB", sl, par)], writes=rres[bi] + [("Sst", i)])

        def run_rr(gens):
            gens = list(gens)
            while gens:
                for g_ in list(gens):
                    try:
                        next(g_)
                    except StopIteration:
                        gens.remove(g_)

        def interleave(gens):
            gens = list(gens)
            while gens:
                for g_ in list(gens):
                    try:
                        yield next(g_)
                    except StopIteration:
                        gens.remove(g_)

        def prompt_pass(main):
            ptiles = [(t0, n) for (t0, n, kind) in STILES if kind == "P"]
            T = len(ptiles)
            seq = [(gi, ti) for gi in range(len(GROUPS)) for ti in range(T)]
            sres = [("Sst", i) for i in range(4)]
            bufs = {}
            entered = {}
            began = {}
            finished = {}
            C0 = {0: 0, 1: 512, 2: 1024}

            def enter(k):
                entered[k] = entered.get(k, 0) + 1
                if entered[k] == 1:
                    gi, ti = seq[k]
                    bufs[k] = load_tile(GROUPS[gi], ptiles[ti][0], ptiles[ti][1], main)

            def begin_group(gi):
                began[gi] = began.get(gi, 0) + 1
                if began[gi] != 1:
                    return
                G = GROUPS[gi]
                V = G["V"]
                c0 = C0[gi]
                if main:
                    S.op("sp", lambda e: e.dma_start(
                        out=Sst[:, :, 0:V], in_=outb.ap()[0:128, c0:c0 + 4 * V].rearrange("p (i v) -> p i v", i=4)),
                        reads=["outb"], writes=sres, dma=True)
                    S.op("dve", lambda e: e.tensor_scalar(out=Sst[:, :, 0:V], in0=Sst[:, :, 0:V], scalar1=flag_t,
                                                          scalar2=None, op0=ALU.mult), reads=["cs"], writes=sres)
                    S.op("act", lambda e: e.activation(out=Sbf[:, :, 0:V], in_=Sst[:, :, 0:V], func=AF.Copy),
                         reads=sres, writes=[("Sbf", i) for i in range(4)])
                else:
                    S.op("dve", lambda e: e.memset(Sst, 0.0), writes=sres)

            def finish_group(gi):
                finished[gi] = finished.get(gi, 0) + 1
                if finished[gi] != 4:
                    return
                G = GROUPS[gi]
                V = G["V"]
                c0 = C0[gi]
                if main:
                    dst = (npA[4 * gi:4 * gi + 4] if gi < 2 else npB).rearrange("i k v -> k i v")
                    S.op("sp", lambda e: e.dma_start(out=dst, in_=Sst[:, :, 0:V]), reads=sres, dma=True)
                else:
                    S.op("sp", lambda e: e.dma_start(
                        out=inb.ap()[:, c0:c0 + 4 * V].rearrange("p (i v) -> p i v", i=4), in_=Sst[:, :, 0:V]),
                        reads=sres, writes=["inb"], dma=True)

            def stream(i):
                def pre(k):
                    gi, ti = seq[k]
                    t0, n = ptiles[ti]
                    yield from head_core(GROUPS[gi], i, bufs[k], n, i, LP, UP, main, k % 2)

                def chain(k):
                    gi, ti = seq[k]
                    G = GROUPS[gi]
                    t0, n = ptiles[ti]
                    if ti == 0:
                        begin_group(gi)
                    blocks = [(0, min(64, n))] + ([(64, n - 64)] if n > 64 else [])
                    if main:
                        for (s0, nb) in blocks:
                            yield from prompt_block(G, i, bufs[k], i, s0, nb, main, k % 2)
                        yield from head_norm(G, i, bufs[k], t0, n, i)
                    else:
                        yield from prepass_tile(G, i, bufs[k], i, blocks, k % 2)
                    if ti == T - 1:
                        finish_group(gi)
                        yield None
                enter(0)
                yield from pre(0)
                for k in range(len(seq)):
                    gens = [chain(k)]
                    if k + 1 < len(seq):
                        enter(k + 1)
                        gens.append(pre(k + 1))
                    yield from interleave(gens)
            run_rr([stream(i) for i in range(4)])

        sbufs = {}

        def sample_stream(G, b, sl):
            V = G["V"]
            nvh = V // 128
            ng = 512 // V
            nq = 16 // ng
            gi = G["g"]
            W = work[sl][0]
            pB = 2 * sl + 1
            bB = ("bank", pB)
            src_t = stA if gi < 2 else stB
            dst_t = nsA if gi < 2 else nsB
            eBl = W["eB"].rearrange("p (j t) -> p j t", t=8)[:, :, 7]
            S0b = sbufs["S0b"][sl]
            Vb = sbufs["Vbk"][sl]
            units = [(i, q) for i in (sl, sl + 2) for q in range(nq)]

            def hd_of(i):
                return (4 * gi + i) if gi < 2 else i

            def load(idx):
                i, q = units[idx]
                r = idx % 2
                sv = sbufs["S0g"][sl][r].rearrange("p (j v) -> p j v", j=ng)
                return S.op("sp", lambda e: e.dma_start(
                    out=sv, in_=src_t[q * ng:(q + 1) * ng, hd_of(i)].rearrange("j k v -> k j v")),
                    writes=[("S0g", sl, r)], dma=True)

            def compute(idx):
                i, q = units[idx]
                r = idx % 2
                j0 = q * ng
                pk = 4 + 2 * sl + r
                S0 = sbufs["S0g"][sl][r]
                sv = S0.rearrange("p (j v) -> p j v", j=ng)
                vtok = vtm[b][:, i * V:(i + 1) * V]
                rS = ("S0g", sl, r)
                yield S.op("act", lambda e: e.activation(out=S0b, in_=S0, func=AF.Copy),
                           reads=[rS], writes=[("S0b", sl)])

                def fj(e):
                    ins = None
                    for jj in range(ng):
                        j = j0 + jj
                        for vh in range(nvh):
                            ins = e.matmul(psum[:, pB, vh * 128 + 8 * j:vh * 128 + 8 * j + 8],
                                           lhsT=S0b[:, jj * V + vh * 128:jj * V + (vh + 1) * 128],
                                           rhs=W["qt"][:, 8 * j:8 * j + 8], start=False,
                                           stop=(q == nq - 1 and jj == ng - 1 and vh == nvh - 1),
                                           skip_group_check=True)
                    return ins
                yield S.op("pe", fj, reads=[("S0b", sl), ("qt", sl, 0)], writes=[bB])
                vb3 = Vb.rearrange("p (j v) -> p j v", j=ng)
                yield S.op("pool", lambda e: e.tensor_tensor(
                    out=vb3, in0=vtok.unsqueeze(1).broadcast_to([128, ng, V]),
                    in1=BM[:, j0:j0 + ng].unsqueeze(2).broadcast_to([128, ng, V]), op=ALU.mult),
                    reads=[("vtm", b), "cs"], writes=[("Vbk", sl)])
                yield S.op("pe", lambda e: e.matmul(psum[:, pk, :], lhsT=W["kdec"], rhs=Vb, start=True, stop=True),
                           reads=[("kdec", sl, 0), ("Vbk", sl)], writes=[("bank", pk)])
                yield S.op("dve", lambda e: e.tensor_tensor(
                    out=sv, in0=sv, in1=eBl[:, j0:j0 + ng].unsqueeze(2).broadcast_to([128, ng, V]), op=ALU.mult),
                    reads=[("eB", sl, 0)], writes=[rS])
                yield S.op("dve", lambda e: e.tensor_tensor(out=S0, in0=S0, in1=psum[:, pk, :], op=ALU.add),
                           writes=[("bank", pk), rS])
                yield S.op("sp", lambda e: e.dma_start(
                    out=dst_t[j0:j0 + ng, hd_of(i)].rearrange("j k v -> k j v"), in_=sv), reads=[rS], dma=True)

            def intra(i):
                vtok = vtm[b][:, i * V:(i + 1) * V]

                def fi(e):
                    ins = None
                    for vh in range(nvh):
                        ins = e.matmul(psum[:, pB, vh * 128:vh * 128 + 128], lhsT=vtok[:, vh * 128:(vh + 1) * 128],
                                       rhs=W["ATm"], start=(vh == 0), stop=False, skip_group_check=True)
                    return ins
                return S.op("pe", fi, reads=[("vtm", b), ("ATm", sl, 0)], writes=[bB])

            yield load(0)
            for idx, (i, q) in enumerate(units):
                if q == 0:
                    yield from head_core(G, i, b, 128, sl, LSm, USm, True)
                    yield intra(i)
                if idx + 1 < len(units):
                    yield load(idx + 1)
                yield from compute(idx)
                if q == nq - 1:
                    yield from head_norm(G, i, b, 0, 128, sl)

        def sample_pass():
            sbufs["S0g"] = [[A2.alloc([128, 512], F32) for _ in range(2)] for _ in range(2)]
            sbufs["S0b"] = [A2.alloc([128, 512], BF16) for _ in range(2)]
            sbufs["Vbk"] = [A2.alloc([128, 512], BF16) for _ in range(2)]
            for G in GROUPS:
                b = load_tile(G, 0, 128, True)
                run_rr([sample_stream(G, b, 0), sample_stream(G, b, 1)])

        lgt = [A.alloc([128, 2, 512], BF16) for _ in range(2)]
        mkb_t = A2.alloc([128, 512], BF16)
        S.op("dve", lambda e: e.tensor_copy(out=mkb_t, in_=cs[:, 128:640]), reads=["cs"], writes=["mkb"])
        for q_, nm_ in enumerate(("LP", "UP", "LS", "US")):
            mkb[nm_] = mkb_t[:, q_ * 128:(q_ + 1) * 128]
        ktm = [A.alloc([128, 512], BF16) for _ in range(2)]
        vtm = [A.alloc([128, 1024], BF16) for _ in range(2)]
        FMt = [A.alloc([128, 16, 128], BF16) for _ in range(2)]
        M1 = A.mark()
        prompt_pass(False)
        if use_cc:
            S.op("pool", lambda e: e.collective_compute(
                "AllGather", ALU.bypass, replica_groups=[[0, 1], [2, 3], [4, 5], [6, 7]],
                ins=[inb.ap().opt()], outs=[outb.ap().opt()]), reads=["inb"], writes=["outb"], special=True)
        else:
            S.op("sp", lambda e: e.dma_start(out=outb.ap()[0:128, :], in_=inb.ap()), reads=["inb"], writes=["outb"],
                 dma=True)
        sample_pass()
        S.barrier()
        A.reset(M1)
        prompt_pass(True)

    wpref = []
    if stage >= 3:
        if stage >= 4:
            wv_ = wmo.rearrange("(k p) n -> p k n", p=128)
            for bb in range(WSLOTS):
                wpref.append(wload(wv_[:, :, bb * 256:(bb + 1) * 256], 16, 256))
        scan_phase()
        S.barrier()
        if stage == 3:
            dbg_of = dout("dbg_of", [128, DC, NT], BF16)
            S.op("sp", lambda e: e.dma_start(out=dbg_of, in_=ofin), dma=True)
            S.barrier()
    A.reset(RX)

    def outproj():
        wv = wmo.rearrange("(k p) n -> p k n", p=128)
        for bb in range(8):
            sl, wb = wpref[bb] if bb < len(wpref) else wload(wv[:, :, bb * 256:(bb + 1) * 256], 16, 256)
            for q in range(2):
                d = 2 * bb + q
                ui, ures = next_unit()
                lin_job(ui, ures, [wb[:, k, q * 128:(q + 1) * 128] for k in range(DC)],
                        [ofin[:, k, :] for k in range(DC)], [("ofin", c) for c in range(DC)] + [("w", sl)])
                S.op("dve", lambda e, ui=ui, d=d: e.tensor_tensor(out=h[:, d, :], in0=unit_flat(ui), in1=h[:, d, :],
                                                                  op=ALU.add),
                     writes=ures + [("h", d)])

    if stage >= 4:
        outproj()
        S.barrier()
        hid = [A.alloc([128, 4, NT], BF16) for _ in range(2)]
        ffn(2, w2i, w2o, hid)
        S.barrier()
        A.reset(RX)

    rmsnorm(3, h, "h")
    yst = [A.alloc([128, D], F32) for _ in range(2)]
    for ti, (t0, n) in enumerate(ttiles):
        yb = yst[ti % 2]
        for g in range(4):
            bank = 6 + (g % 2)

            def f_tr(e, t0=t0, n=n, g=g, bank=bank):
                ins = None
                for q in range(4):
                    c = 4 * g + q
                    ins = e.transpose(out=psum[0:n, bank, q * 128:(q + 1) * 128], in_=h[:, c, t0:t0 + n],
                                      identity=ident)
                return ins
            S.op("pe", f_tr, reads=[("h", c) for c in range(4 * g, 4 * g + 4)] + ["cs"], writes=[("bank", bank)])
            S.op("act", lambda e, yb=yb, n=n, g=g, bank=bank: e.activation(
                out=yb[0:n, g * 512:(g + 1) * 512], in_=psum[0:n, bank, :], func=AF.Copy),
                writes=[("bank", bank), ("yst", ti % 2, g)])
        S.op("sp", lambda e, yb=yb, t0=t0, n=n: e.dma_start(out=y[t0:t0 + n, :], in_=yb[0:n, :]),
             reads=[("yst", ti % 2, g) for g in range(4)], dma=True)

    block = es.enter_context(nc.Block())
    S.emit(nc, block, es)
    es.close()
    return nc


def make_consts(flag=0.0):
    c = np.zeros((128, 672), np.float32)
    c[:, 0:128] = np.eye(128, dtype=np.float32)
    idx = np.arange(128)
    for (o, blk) in ((128, 64), (384, 8)):
        same = (idx[:, None] // blk) == (idx[None, :] // blk)
        c[:, o:o + 128] = (same & (idx[:, None] <= idx[None, :])).astype(np.float32)
        c[:, o + 128:o + 256] = (same & (idx[:, None] > idx[None, :])).astype(np.float32)
    c[:, 640:656] = ((idx[:, None] // 8) == np.arange(16)[None, :]).astype(np.float32)
    c[:, 656] = flag
    return c


_CACHE = {}


def kernel(x_prompt, x_sample, state_hgrn, state_gla, meta_tokens, lb_logits,
           ffn1_norm, w_ffn1_in, w_ffn1_out, mix_norm, w_in, w_alpha_up, b_alpha, gnorm_a, gnorm_b, w_out,
           ffn2_norm, w_ffn2_in, w_ffn2_out, final_norm):
    f = lambda a: np.ascontiguousarray(np.asarray(a, dtype=np.float32))
    x_prompt, x_sample, state_hgrn, state_gla = f(x_prompt), f(x_sample), f(state_hgrn), f(state_gla)
    meta = f(meta_tokens)
    if "nc" not in _CACHE:
        _CACHE["nc"] = build_program()
    nc = _CACHE["nc"]
    shared = dict(
        nrm=f(np.stack([f(ffn1_norm)[0], f(mix_norm)[0], f(ffn2_norm)[0], f(final_norm)])),
        w1i=f(w_ffn1_in)[0], w1o=f(w_ffn1_out)[0], w2i=f(w_ffn2_in)[0], w2o=f(w_ffn2_out)[0],
        wmi=f(w_in)[0], wal=f(np.concatenate([f(w_alpha_up)[0], f(b_alpha)[0][None, :]], axis=0)),
        lbl=f(lb_logits), gn=f(np.stack([f(gnorm_a)[0], f(gnorm_b)[0]])), wmo=f(w_out)[0])
    half = 1016
    in_maps = []
    for c in range(8):
        b, odd = c // 2, c % 2
        xs = x_sample[16 * c:16 * c + 16].reshape(128, D)
        if odd:
            xp = x_prompt[b, half:]
        else:
            xp = np.concatenate([meta, x_prompt[b, :half]], axis=0)
        m = dict(shared)
        m["xin"] = f(np.concatenate([xs, xp], axis=0))
        m["cst"] = make_consts(float(odd))
        m["stA"] = f(state_hgrn[0, 16 * c:16 * c + 16])
        m["stB"] = f(state_gla[0, 16 * c:16 * c + 16])
        in_maps.append(m)
    res = run_bass_kernel_spmd(nc, in_maps, core_ids=list(range(8)))
    R = res.results
    y_prompt = np.empty((4, 2048, D), np.float32)
    y_sample = np.empty((128, 8, D), np.float32)
    na_p = np.empty((1, 4, 8, 128, 128), np.float32)
    nb_p = np.empty((1, 4, 4, 128, 256), np.float32)
    na_s = np.empty((1, 128, 8, 128, 128), np.float32)
    nb_s = np.empty((1, 128, 4, 128, 256), np.float32)
    for c in range(8):
        b, odd = c // 2, c % 2
        yk = np.asarray(R[c]["y"], np.float32)
        y_sample[16 * c:16 * c + 16] = yk[0:128].reshape(16, 8, D)
        if odd:
            y_prompt[b, half:] = yk[128:]
            na_p[0, b] = np.asarray(R[c]["npA"], np.float32)
            nb_p[0, b] = np.asarray(R[c]["npB"], np.float32)
        else:
            y_prompt[b, :half] = yk[128 + N_META:]
        na_s[0, 16 * c:16 * c + 16] = np.asarray(R[c]["nsA"], np.float32)
        nb_s[0, 16 * c:16 * c + 16] = np.asarray(R[c]["nsB"], np.float32)
    return (y_prompt, y_sample, na_p, nb_p, na_s, nb_s)
```
